# Optimizing a Trainium2 kernel written in Bass

```python
import jax, jax.numpy as jnp
from jax import lax
import numpy as np

D_MODEL = 1024
BATCH = 8
SEQ = 2048
DEPTH = 4
DEC_BATCH = 128
DEC_SEQ = 1
PAST_LEN = 16384
PAGE_SIZE = 128

N_META = 16
N_EVEN = (DEPTH + 1) // 2
N_ODD = DEPTH // 2
A_HEADS = 4
A_DK = 128
A_DV = 128
A_W = A_HEADS * A_DK
B_HEADS = 8
B_N = 64
B_W = B_HEADS * B_N
B_DECAY_LORA = 64
B_AAA_LORA = 64
B_GATE_LORA = 128
B_PROJ = 3 * B_W + B_DECAY_LORA + B_AAA_LORA + B_GATE_LORA
EVEN_PROJ = 4 * A_W + B_PROJ
EVEN_MIX = A_W + B_W
RWKV_GN_EPS = 64e-5
C_HEADS = 4
C_DK = 128
C_DV = 256
C_KW = C_HEADS * C_DK
C_VW = C_HEADS * C_DV
C_GATE_LORA = 16
C_GATE_TAU = 16.0
ODD_PROJ = 2 * C_KW + 2 * C_VW + C_GATE_LORA
D_FF = 2816
CHUNK = 32
RMS_EPS = 1e-6

kernel_name = 'hgrn2_rwkv7_gla_macaron_step'


def rmsnorm(x, g):
    xf = x.astype(jnp.float32)
    y = xf * lax.rsqrt(jnp.mean(xf * xf, axis=-1, keepdims=True) + RMS_EPS)
    return (y * g.astype(jnp.float32)).astype(x.dtype)


def half_ffn(x, g, w_gate, w_up, w_down):
    h = rmsnorm(x, g)
    return 0.5 * ((jax.nn.silu(h @ w_gate) * (h @ w_up)) @ w_down)


def split_heads(z, n_heads):
    return z.reshape(z.shape[0], z.shape[1], n_heads, -1)


def hgrn_lower_bounds(lb_param):
    p = jax.nn.softmax(lb_param.astype(jnp.float32), axis=0)
    c = jnp.cumsum(p, axis=0)
    return c - c[0:1]


def gla_chunked(q, k, v, logf, s0, chunk):
    b, t, h, dk = q.shape
    dv = v.shape[-1]
    c = chunk if t % chunk == 0 else t
    n = t // c

    def split(z):
        return jnp.moveaxis(z.reshape(b, n, c, *z.shape[2:]), 1, 0)

    causal = jnp.tril(jnp.ones((c, c), bool))[None, :, :, None, None]

    def step(s, inp):
        qc, kc, vc, lc = (z.astype(jnp.float32) for z in inp)
        cum = jnp.cumsum(lc, axis=1)
        diff = cum[:, :, None] - cum[:, None, :]
        decay = jnp.exp(jnp.where(causal, diff, -jnp.inf))
        scores = jnp.einsum('bthk,bshk,btshk->bhts', qc, kc, decay)
        o_intra = jnp.einsum('bhts,bshv->bthv', scores, vc)
        o_inter = jnp.einsum('bthk,bhkv->bthv', qc * jnp.exp(cum), s)
        last = cum[:, -1]
        k_dec = kc * jnp.exp(last[:, None] - cum)
        s_new = jnp.exp(last)[..., None] * s + jnp.einsum('bshk,bshv->bhkv', k_dec, vc)
        return s_new, o_intra + o_inter

    s_fin, o = lax.scan(step, s0.astype(jnp.float32), (split(q), split(k), split(v), split(logf)))
    o = jnp.moveaxis(o, 0, 1).reshape(b, t, h, dv)
    return o.astype(v.dtype), s_fin


def gla_seq(q, k, v, logf, s0, lead):
    if lead == 0:
        return gla_chunked(q, k, v, logf, s0, CHUNK)
    o1, s1 = gla_chunked(q[:, :lead], k[:, :lead], v[:, :lead], logf[:, :lead], s0, lead)
    o2, s2 = gla_chunked(q[:, lead:], k[:, lead:], v[:, lead:], logf[:, lead:], s1, CHUNK)
    return jnp.concatenate([o1, o2], axis=1), s2


def rwkv7_scan(r, w, k, v, kk, a, s0):
    def step(s, inp):
        rt, wt, kt, vt, kkt, at = inp
        sa = jnp.einsum('bhij,bhj->bhi', s, -kkt)
        s = s * wt[:, :, None, :] + sa[..., None] * (kkt * at)[:, :, None, :] + vt[..., None] * kt[:, :, None, :]
        y = jnp.einsum('bhij,bhj->bhi', s, rt)
        return s, y

    xs = tuple(jnp.moveaxis(z, 1, 0) for z in (r, w, k, v, kk, a))
    s_fin, y = lax.scan(step, s0.astype(jnp.float32), xs)
    return jnp.moveaxis(y, 0, 1), s_fin


def even_mixer(h, lb, s_hgrn, s_rwkv, shift_prev, prm, e, lead):
    f32 = jnp.float32
    bsz, t, _ = h.shape
    proj = (h @ prm['even_w_in'][e]).astype(f32)
    qa, fa, ia, ga, pb = jnp.split(proj, [A_W, 2 * A_W, 3 * A_W, 4 * A_W], axis=-1)
    ka = (1.0 - lb) * jax.nn.sigmoid(-fa)
    logf = jnp.logaddexp(jnp.log(lb), jnp.log1p(-lb) + jax.nn.log_sigmoid(fa))
    oa, s_hgrn_new = gla_seq(split_heads(qa * A_DK ** -0.5, A_HEADS), split_heads(ka, A_HEADS),
                             split_heads(jax.nn.silu(ia), A_HEADS), split_heads(logf, A_HEADS), s_hgrn, lead)
    oa = rmsnorm(oa.reshape(bsz, t, A_W) * jax.nn.sigmoid(ga), prm['hgrn_norm'][e])
    prev = jnp.concatenate([shift_prev[:, None].astype(f32), pb[:, :-1]], axis=1)
    xm = pb + (prev - pb) * prm['rwkv_mu'][e]
    xr, xk, xv, xw, xa, xg = jnp.split(
        xm, [B_W, 2 * B_W, 3 * B_W, 3 * B_W + B_DECAY_LORA, 3 * B_W + B_DECAY_LORA + B_AAA_LORA], axis=-1)
    wv = prm['rwkv_w0'][e] + jnp.tanh(xw) @ prm['rwkv_w2'][e]
    decay = jnp.exp(-jnp.exp(-jax.nn.softplus(-wv) - 0.5))
    a = jax.nn.sigmoid(prm['rwkv_a0'][e] + xa @ prm['rwkv_a2'][e])
    g = jax.nn.sigmoid(xg) @ prm['rwkv_g2'][e]
    kk = split_heads(xk * prm['rwkv_kk'][e], B_HEADS)
    kk = kk / jnp.maximum(jnp.sqrt(jnp.sum(kk * kk, axis=-1, keepdims=True)), 1e-12)
    kmod = xk * (1.0 + (a - 1.0) * prm['rwkv_ka'][e])
    r4, k4, v4 = split_heads(xr, B_HEADS), split_heads(kmod, B_HEADS), split_heads(xv, B_HEADS)
    y, s_rwkv_new = rwkv7_scan(r4, split_heads(decay, B_HEADS), k4, v4, kk, split_heads(a, B_HEADS), s_rwkv)
    mu = jnp.mean(y, axis=-1, keepdims=True)
    var = jnp.mean(jnp.square(y - mu), axis=-1, keepdims=True)
    yn = ((y - mu) * lax.rsqrt(var + RWKV_GN_EPS)).reshape(bsz, t, B_W) * prm['rwkv_ln_w'][e] + prm['rwkv_ln_b'][e]
    bonus = jnp.sum(r4 * k4 * prm['rwkv_rk'][e].reshape(B_HEADS, B_N), axis=-1, keepdims=True) * v4
    ob = (yn + bonus.reshape(bsz, t, B_W)) * g
    mix = jnp.concatenate([oa.astype(f32), ob], axis=-1).astype(h.dtype) @ prm['even_w_out'][e]
    return mix, s_hgrn_new, s_rwkv_new, pb[:, -1]


def odd_mixer(h, s_gla, prm, o, lead):
    f32 = jnp.float32
    bsz, t, _ = h.shape
    proj = (h @ prm['odd_w_in'][o]).astype(f32)
    qc, kc, vc, gd, og = jnp.split(
        proj, [C_KW, 2 * C_KW, 2 * C_KW + C_VW, 2 * C_KW + C_VW + C_GATE_LORA], axis=-1)
    gk = gd @ prm['gla_gate_up'][o] + prm['gla_gate_b'][o]
    logf = jax.nn.log_sigmoid(gk) / C_GATE_TAU
    oc, s_new = gla_seq(split_heads(qc * C_DK ** -0.5, C_HEADS), split_heads(kc, C_HEADS),
                        split_heads(vc, C_HEADS), split_heads(logf, C_HEADS), s_gla, lead)
    oc = rmsnorm(oc, prm['gla_norm'][o]).reshape(bsz, t, C_VW) * jax.nn.silu(og)
    return oc.astype(h.dtype) @ prm['odd_w_out'][o], s_new


def forward(x, st_hgrn, st_rwkv, st_shift, st_gla, prm, lead):
    lbs = hgrn_lower_bounds(prm['hgrn_lb'])
    new_h, new_r, new_s, new_g = [], [], [], []
    for l in range(DEPTH):
        x = x + half_ffn(x, prm['norm_ffn1'][l], prm['ffn1_gate'][l], prm['ffn1_up'][l], prm['ffn1_down'][l]).astype(x.dtype)
        h = rmsnorm(x, prm['norm_mix'][l])
        if l % 2 == 0:
            e = l // 2
            m, sh, sr, ss = even_mixer(h, lbs[e], st_hgrn[e], st_rwkv[e], st_shift[e], prm, e, lead)
            new_h.append(sh)
            new_r.append(sr)
            new_s.append(ss)
        else:
            o = l // 2
            m, sg = odd_mixer(h, st_gla[o], prm, o, lead)
            new_g.append(sg)
        x = x + m.astype(x.dtype)
        x = x + half_ffn(x, prm['norm_ffn2'][l], prm['ffn2_gate'][l], prm['ffn2_up'][l], prm['ffn2_down'][l]).astype(x.dtype)
    y = rmsnorm(x, prm['final_norm'])
    return y, jnp.stack(new_h), jnp.stack(new_r), jnp.stack(new_s), jnp.stack(new_g)


def setup_inputs(seed: int = 0) -> dict:
    key = jax.random.key(seed)
    ks = iter(jax.random.split(key, 64))
    f32 = jnp.float32

    def nrm(shape, scale):
        return jax.random.normal(next(ks), shape, f32) * scale

    def unif(shape, lo, hi):
        return jax.random.uniform(next(ks), shape, f32, lo, hi)

    d = D_MODEL
    return {
        'x_prompt': nrm((BATCH, SEQ, d), 1.0),
        'x_sample': nrm((DEC_BATCH, DEC_SEQ, d), 1.0),
        'state_hgrn': nrm((N_EVEN, DEC_BATCH, A_HEADS, A_DK, A_DV), 0.5),
        'state_rwkv': nrm((N_EVEN, DEC_BATCH, B_HEADS, B_N, B_N), 0.3),
        'state_rwkv_shift': nrm((N_EVEN, DEC_BATCH, B_PROJ), 1.0),
        'state_gla': nrm((N_ODD, DEC_BATCH, C_HEADS, C_DK, C_DV), 1.0),
        'meta_tokens': nrm((N_META, d), 1.0),
        'norm_ffn1': 1.0 + nrm((DEPTH, d), 0.02),
        'ffn1_gate': nrm((DEPTH, d, D_FF), d ** -0.5),
        'ffn1_up': nrm((DEPTH, d, D_FF), d ** -0.5),
        'ffn1_down': nrm((DEPTH, D_FF, d), D_FF ** -0.5),
        'norm_mix': 1.0 + nrm((DEPTH, d), 0.02),
        'even_w_in': nrm((N_EVEN, d, EVEN_PROJ), d ** -0.5),
        'hgrn_lb': nrm((N_EVEN, A_W), 0.1),
        'hgrn_norm': 1.0 + nrm((N_EVEN, A_W), 0.02),
        'rwkv_mu': unif((N_EVEN, B_PROJ), 0.0, 1.0),
        'rwkv_w0': unif((N_EVEN, B_W), -5.0, 0.5),
        'rwkv_w2': nrm((N_EVEN, B_DECAY_LORA, B_W), 0.1),
        'rwkv_a0': nrm((N_EVEN, B_W), 0.1),
        'rwkv_a2': nrm((N_EVEN, B_AAA_LORA, B_W), 0.1),
        'rwkv_g2': nrm((N_EVEN, B_GATE_LORA, B_W), B_GATE_LORA ** -0.5),
        'rwkv_kk': 0.85 + nrm((N_EVEN, B_W), 0.02),
        'rwkv_ka': 1.0 + nrm((N_EVEN, B_W), 0.02),
        'rwkv_rk': nrm((N_EVEN, B_W), 0.1),
        'rwkv_ln_w': 1.0 + nrm((N_EVEN, B_W), 0.02),
        'rwkv_ln_b': nrm((N_EVEN, B_W), 0.02),
        'even_w_out': nrm((N_EVEN, EVEN_MIX, d), EVEN_MIX ** -0.5),
        'odd_w_in': nrm((N_ODD, d, ODD_PROJ), d ** -0.5),
        'gla_gate_up': nrm((N_ODD, C_GATE_LORA, C_KW), C_GATE_LORA ** -0.5),
        'gla_gate_b': nrm((N_ODD, C_KW), 0.1),
        'gla_norm': 1.0 + nrm((N_ODD, C_DV), 0.02),
        'odd_w_out': nrm((N_ODD, C_VW, d), C_VW ** -0.5),
        'norm_ffn2': 1.0 + nrm((DEPTH, d), 0.02),
        'ffn2_gate': nrm((DEPTH, d, D_FF), d ** -0.5),
        'ffn2_up': nrm((DEPTH, d, D_FF), d ** -0.5),
        'ffn2_down': nrm((DEPTH, D_FF, d), D_FF ** -0.5),
        'final_norm': 1.0 + nrm((d,), 0.02),
    }


def reference(x_prompt, x_sample, state_hgrn, state_rwkv, state_rwkv_shift, state_gla, meta_tokens,
              norm_ffn1, ffn1_gate, ffn1_up, ffn1_down, norm_mix, even_w_in, hgrn_lb, hgrn_norm,
              rwkv_mu, rwkv_w0, rwkv_w2, rwkv_a0, rwkv_a2, rwkv_g2, rwkv_kk, rwkv_ka, rwkv_rk,
              rwkv_ln_w, rwkv_ln_b, even_w_out, odd_w_in, gla_gate_up, gla_gate_b, gla_norm, odd_w_out,
              norm_ffn2, ffn2_gate, ffn2_up, ffn2_down, final_norm):
    prm = dict(norm_ffn1=norm_ffn1, ffn1_gate=ffn1_gate, ffn1_up=ffn1_up, ffn1_down=ffn1_down,
               norm_mix=norm_mix, even_w_in=even_w_in, hgrn_lb=hgrn_lb, hgrn_norm=hgrn_norm,
               rwkv_mu=rwkv_mu, rwkv_w0=rwkv_w0, rwkv_w2=rwkv_w2, rwkv_a0=rwkv_a0, rwkv_a2=rwkv_a2,
               rwkv_g2=rwkv_g2, rwkv_kk=rwkv_kk, rwkv_ka=rwkv_ka, rwkv_rk=rwkv_rk,
               rwkv_ln_w=rwkv_ln_w, rwkv_ln_b=rwkv_ln_b, even_w_out=even_w_out, odd_w_in=odd_w_in,
               gla_gate_up=gla_gate_up, gla_gate_b=gla_gate_b, gla_norm=gla_norm, odd_w_out=odd_w_out,
               norm_ffn2=norm_ffn2, ffn2_gate=ffn2_gate, ffn2_up=ffn2_up, ffn2_down=ffn2_down,
               final_norm=final_norm)
    f32 = jnp.float32
    bsz = x_prompt.shape[0]
    meta = jnp.broadcast_to(meta_tokens.astype(x_prompt.dtype)[None], (bsz, N_META, D_MODEL))
    xp = jnp.concatenate([meta, x_prompt], axis=1)
    z_h = jnp.zeros((N_EVEN, bsz, A_HEADS, A_DK, A_DV), f32)
    z_r = jnp.zeros((N_EVEN, bsz, B_HEADS, B_N, B_N), f32)
    z_s = jnp.zeros((N_EVEN, bsz, B_PROJ), f32)
    z_g = jnp.zeros((N_ODD, bsz, C_HEADS, C_DK, C_DV), f32)
    yp, hp, rp, sp, gp = forward(xp, z_h, z_r, z_s, z_g, prm, N_META)
    ys, hs, rs, ss, gs = forward(x_sample, state_hgrn, state_rwkv, state_rwkv_shift, state_gla, prm, 0)
    return (yp[:, N_META:], ys, hp, rp, sp, gp, hs, rs, ss, gs)
```

```python
import contextlib
import numpy as np
import concourse.bass as bass
import concourse.mybir as mybir
from concourse.bass_utils import run_bass_kernel_spmd

F32 = mybir.dt.float32
BF16 = mybir.dt.bfloat16
ALU = mybir.AluOpType
AF = mybir.ActivationFunctionType
AX = mybir.AxisListType

ENGINES = ("pe", "act", "dve", "pool", "sp")
EPOCH = 30000

D = 1024
KD = 8
DFF = 2816
PAD = 48
NMETA = 16
CH = 64
RMS_EPS = 1e-6
GN_EPS = 64e-5
DEBUG_STAGE = 9


class Sched:
    def __init__(self, nc):
        self.nc = nc
        self.ops = {e: [] for e in ENGINES}
        self.seq = {e: 0 for e in ENGINES}
        self.res = {}
        self.waited = {e: {} for e in ENGINES}
        self.dma_cnt = {}
        self.sem_handles = {}

    def _tok_for_seq(self, eng, seq):
        return (("e", eng, (seq - 1) // EPOCH), (seq - 1) % EPOCH + 1)

    def op(self, eng, fn, reads=(), writes=(), dma=None):
        deps = {}

        def add(tok):
            if tok is None:
                return
            k, v = tok
            if deps.get(k, 0) < v:
                deps[k] = v

        for r in reads:
            st = self.res.get(r)
            if st is not None:
                add(st["w"])
        for w in writes:
            st = self.res.get(w)
            if st is not None:
                add(st["w"])
                for t in st["r"].items():
                    add(t)
        waits = []
        for k, v in deps.items():
            if k[0] == "e" and k[1] == eng and eng == "pe":
                continue
            if self.waited[eng].get(k, 0) >= v:
                continue
            self.waited[eng][k] = v
            waits.append((k, v))
        if dma is not None:
            key = ("d", dma)
            self.dma_cnt[dma] = self.dma_cnt.get(dma, 0) + 16
            tok = (key, self.dma_cnt[dma])
            inc = (key, 16)
        else:
            self.seq[eng] += 1
            tok = self._tok_for_seq(eng, self.seq[eng])
            inc = (tok[0], 1)
        self.ops[eng].append((waits, fn, inc))
        for w in writes:
            self.res[w] = {"w": tok, "r": {}}
        for r in reads:
            if r in writes:
                continue
            st = self.res.setdefault(r, {"w": None, "r": {}})
            if st["r"].get(tok[0], 0) < tok[1]:
                st["r"][tok[0]] = tok[1]
        return tok

    def barrier(self):
        toks = []
        for name, cnt in self.dma_cnt.items():
            toks.append((("d", name), cnt))
        for e in ENGINES:
            if self.seq[e] > 0:
                toks.append(self._tok_for_seq(e, self.seq[e]))
        for e in ENGINES:
            waits = []
            for k, v in toks:
                if k[0] == "e" and k[1] == e:
                    continue
                if self.waited[e].get(k, 0) >= v:
                    continue
                self.waited[e][k] = v
                waits.append((k, v))
            if waits:
                self.ops[e].append((waits, None, None))

    def emit(self, stack):
        nc = self.nc
        self.barrier()
        keys = set()
        for e in ENGINES:
            for waits, fn, inc in self.ops[e]:
                if inc is not None:
                    keys.add(inc[0])
                for k, v in waits:
                    keys.add(k)
        for k in sorted(keys, key=str):
            nm = "s_" + "_".join(str(x) for x in k)
            self.sem_handles[k] = stack.enter_context(nc.semaphore(nm))
        block = stack.enter_context(nc.Block())
        Hd = self.sem_handles

        def replay(ename):
            def body(eng):
                for waits, fn, inc in self.ops[ename]:
                    for k, v in waits:
                        eng.wait_ge(Hd[k], v)
                    if fn is None:
                        continue
                    ins = fn(eng)
                    ins.then_inc(Hd[inc[0]], inc[1])
            return body

        block.tensor(replay("pe"))
        block.scalar(replay("act"))
        block.vector(replay("dve"))
        block.gpsimd(replay("pool"))
        block.sync(replay("sp"))


PP_SPEC = [
    ("norm_ffn1", "L", 8), ("norm_mix", "L", 8), ("norm_ffn2", "L", 8), ("final_norm", None, 8),
    ("hgrn_lb", "E", 4), ("hgrn_norm", "E", 4), ("rwkv_mu", "E", 14),
    ("rwkv_w0", "E", 4), ("rwkv_a0", "E", 4), ("rwkv_kk", "E", 4), ("rwkv_ka", "E", 4),
    ("rwkv_rk", "E", 4), ("rwkv_ln_w", "E", 4), ("rwkv_ln_b", "E", 4),
    ("gla_gate_b", "O", 4), ("gla_norm", "O", 2),
]


def pp_layout(depth):
    n_even = (depth + 1) // 2
    n_odd = depth // 2
    off = {}
    o = 0
    for name, kind, nch in PP_SPEC:
        cnt = {"L": depth, "E": n_even, "O": n_odd, None: 1}[kind]
        off[name] = (o, nch)
        o += cnt * nch
    return off, o


def pack_pp(inputs, depth):
    off, n = pp_layout(depth)
    pp = np.zeros((128, n), np.float32)
    for name, kind, nch in PP_SPEC:
        a = np.asarray(inputs[name], np.float32)
        a = a.reshape(-1, nch, 128)
        o = off[name][0]
        pp[:, o:o + a.shape[0] * nch] = a.transpose(2, 0, 1).reshape(128, -1)
    return pp


CST_SPEC = [("ident", 128), ("ones", 128), ("causT", 64), ("mt2", 256), ("mstr", 128), ("idt", 64), ("blk", 128),
            ("sel", 16 * 128)]
NCF = 128 + 128 + 64 + 256 + 128 + 64


def cst_layout():
    off = {}
    o = 0
    for name, n in CST_SPEC:
        off[name] = o
        o += n
    return off, o


def make_cst():
    off, n = cst_layout()
    c = np.zeros((128, n), np.float32)
    c[:, off["ident"]:off["ident"] + 128] = np.eye(128)
    c[:, off["ones"]:off["ones"] + 128] = 1.0
    s = np.arange(64)
    causT = (s[:, None] <= s[None, :]).astype(np.float32)
    c[:64, off["causT"]:off["causT"] + 64] = causT
    strT = (s[:, None] < s[None, :]).astype(np.float32)
    mstrT = np.zeros((128, 128), np.float32)
    mincT = np.zeros((128, 128), np.float32)
    blk = np.zeros((128, 128), np.float32)
    for h in range(2):
        mstrT[h * 64:(h + 1) * 64, h * 64:(h + 1) * 64] = strT
        mincT[h * 64:(h + 1) * 64, h * 64:(h + 1) * 64] = causT
        blk[h * 64:(h + 1) * 64, h * 64:(h + 1) * 64] = 1.0
    c[:, off["mt2"]:off["mt2"] + 128] = mstrT
    c[:, off["mt2"] + 128:off["mt2"] + 256] = mincT
    c[:, off["mstr"]:off["mstr"] + 128] = mstrT.T
    c[:, off["blk"]:off["blk"] + 128] = blk
    c[:64, off["idt"]:off["idt"] + 64] = np.eye(64)
    c[64:, off["idt"]:off["idt"] + 64] = np.eye(64)
    sel = np.zeros((128, 16, 128), np.float32)
    for b in range(16):
        sel[b, b, :] = 1.0
    c[:, off["sel"]:off["sel"] + 16 * 128] = sel.reshape(128, -1)
    return c


WEIGHT_SHAPES = {
    "ffn1_gate": lambda L: [L, D, DFF], "ffn1_up": lambda L: [L, D, DFF], "ffn1_down": lambda L: [L, DFF, D],
    "ffn2_gate": lambda L: [L, D, DFF], "ffn2_up": lambda L: [L, D, DFF], "ffn2_down": lambda L: [L, DFF, D],
}


def build(S_LEN=2048, NS=16, DEPTH=4, do_even=True, do_odd=True, do_ffn=True):
    NE = (DEPTH + 1) // 2
    NO = DEPTH // 2
    TP = PAD + NMETA + S_LEN
    assert TP % CH == 0 and S_LEN % 128 == 0
    NCHK = TP // CH
    TT = TP + NS
    tiles = []
    t = 0
    while t < TT:
        n = min(512, TT - t)
        tiles.append((t, n))
        t += n
    NT = len(tiles)
    ppo, NPP = pp_layout(DEPTH)
    cso, NCST = cst_layout()

    nc = bass.Bass("TRN2", target_bir_lowering=False)

    def din(name, shape):
        return nc.dram_tensor(name, list(shape), F32, kind="ExternalInput")

    def dout(name, shape):
        return nc.dram_tensor(name, list(shape), F32, kind="ExternalOutput")

    xp_d = din("xp", [S_LEN, D])
    xs_d = din("xs", [NS, D])
    meta_d = din("meta", [NMETA, D])
    pp_d = din("pp", [128, NPP])
    cst_d = din("cst", [128, NCST])
    W = {}
    for nm in ("ffn1_gate", "ffn1_up", "ffn2_gate", "ffn2_up"):
        W[nm] = din(nm, [DEPTH, D, DFF])
    for nm in ("ffn1_down", "ffn2_down"):
        W[nm] = din(nm, [DEPTH, DFF, D])
    W["even_w_in"] = din("even_w_in", [NE, D, 3840])
    W["even_w_out"] = din("even_w_out", [NE, 1024, D])
    W["odd_w_in"] = din("odd_w_in", [max(NO, 1), D, 3088])
    W["odd_w_out"] = din("odd_w_out", [max(NO, 1), 1024, D])
    W["rwkv_w2"] = din("rwkv_w2", [NE, 64, 512])
    W["rwkv_a2"] = din("rwkv_a2", [NE, 64, 512])
    W["rwkv_g2"] = din("rwkv_g2", [NE, 128, 512])
    W["gla_gate_up"] = din("gla_gate_up", [max(NO, 1), 16, 512])
    sth_d = din("st_hgrn", [NE, NS, 4, 128, 128])
    str_d = din("st_rwkv", [NE, NS, 8, 64, 64])
    sts_d = din("st_shift", [NE, NS, 1792])
    stg_d = din("st_gla", [max(NO, 1), NS, 4, 128, 256])

    yp_d = dout("y_prompt", [S_LEN, D])
    ys_d = dout("y_sample", [NS, D])
    hp_d = dout("hgrn_prompt", [NE, 4, 128, 128])
    rp_d = dout("rwkv_prompt", [NE, 8, 64, 64])
    sp_d = dout("shift_prompt", [NE, 1792])
    gp_d = dout("gla_prompt", [max(NO, 1), 4, 128, 256])
    hs_d = dout("hgrn_sample", [NE, NS, 4, 128, 128])
    rs_d = dout("rwkv_sample", [NE, NS, 8, 64, 64])
    ss_d = dout("shift_sample", [NE, NS, 1792])
    gs_d = dout("gla_sample", [max(NO, 1), NS, 4, 128, 256])

    with contextlib.ExitStack() as st:
        _uid = [0]

        def sb(name, shape, dt, stack=st):
            _uid[0] += 1
            return stack.enter_context(nc.sbuf_tensor("%s_%d" % (name, _uid[0]), list(shape), dt))

        X = sb("X", [128, KD, TT], F32)
        Hb = sb("Hb", [128, KD, TT + 112], BF16)
        MW = sb("MW", [128, 17024], BF16)
        FW = sb("FW", [128, 2, 6144], BF16)
        PPt = sb("PPt", [128, NPP], F32)
        CSTf = sb("CSTf", [128, NCF], F32)
        CSTb = sb("CSTb", [128, NCST], BF16)
        ONESf = sb("ONESf", [128, 64], F32)
        MIXT = sb("MIXT", [128, 8, 512], BF16)
        PS = [st.enter_context(nc.psum_tensor("PS%d" % i, [128, 512], F32)) for i in range(8)]
        PSb = [p.bitcast(BF16) for p in PS]

        S = Sched(nc)

        def cb(name, n=None, rows=128):
            o = cso[name]
            if n is None:
                n = dict(CST_SPEC)[name]
            return CSTb[0:rows, o:o + n]

        def cf(name, n=None, rows=128):
            o = cso[name]
            if n is None:
                n = dict(CST_SPEC)[name]
            return CSTf[0:rows, o:o + n]

        def ppc(name, idx, c0=0, n=None):
            o, nch = ppo[name]
            if n is None:
                n = nch
            return PPt[:, o + idx * nch + c0:o + idx * nch + c0 + n]

        def act(out, in_, func, reads, writes, bias=0.0, scale=1.0):
            S.op("act", lambda e: e.activation(out=out, in_=in_, func=func, bias=bias, scale=scale), reads, writes)

        def tt(out, a, b, op, reads, writes, eng="dve"):
            S.op(eng, lambda e: e.tensor_tensor(out=out, in0=a, in1=b, op=op), reads, writes)

        def ts(out, a, s1, s2, op0, op1, reads, writes, eng="dve"):
            if s2 is None:
                S.op(eng, lambda e: e.tensor_scalar(out=out, in0=a, scalar1=s1, scalar2=None, op0=op0), reads, writes)
            else:
                S.op(eng, lambda e: e.tensor_scalar(out=out, in0=a, scalar1=s1, scalar2=s2, op0=op0, op1=op1),
                     reads, writes)

        def stt(out, a, s, b, op0, op1, reads, writes):
            S.op("dve", lambda e: e.scalar_tensor_tensor(out=out, in0=a, scalar=s, in1=b, op0=op0, op1=op1),
                 reads, writes)

        def cp(eng, out, in_, reads, writes):
            if eng == "act":
                S.op("act", lambda e: e.activation(out=out, in_=in_, func=AF.Copy), reads, writes)
            else:
                S.op(eng, lambda e: e.tensor_copy(out=out, in_=in_), reads, writes)

        def mm(out, pairs, reads, writes):
            def fn(e):
                n = len(pairs)
                ins = None
                for i, (l, r) in enumerate(pairs):
                    ins = e.matmul(out, l, r, start=(i == 0), stop=(i == n - 1))
                return ins
            S.op("pe", fn, reads, writes)

        def tr(out, in_, ident, reads, writes):
            S.op("pe", lambda e: e.transpose(out, in_, ident), reads, writes)

        def dma(eng, out, in_, reads, writes, sem):
            S.op(eng, lambda e: e.dma_start(out=out, in_=in_), reads, writes, dma=sem)

        def memset(eng, ap, val, writes):
            S.op(eng, lambda e: e.memset(ap, val), (), writes)

        def sigmoid_chain(out, in_, tmp, reads, writes, tmpname, scale_in=1.0):
            act(tmp, in_, AF.Exp, reads, [tmpname], scale=-scale_in)
            act(tmp, tmp, AF.Ln, [tmpname], [tmpname], bias=1.0)
            act(out, tmp, AF.Exp, [tmpname], writes, scale=-1.0)

        dma("sp", PPt[:], pp_d.ap(), [], ["PP"], "PP")
        dma("sp", CSTf[:], cst_d.ap()[:, 0:NCF], [], ["CSTf"], "CSTf")
        for c0 in range(0, NCST, 1024):
            c1 = min(NCST, c0 + 1024)
            dma("pool", CSTb[:, c0:c1], cst_d.ap()[:, c0:c1], [], ["CSTb"], "CSTb")
        memset("dve", ONESf[:], 1.0, ["ONESf"])
        memset("dve", X[:, :, 0:PAD], 0.0, ["X0"])
        memset("dve", Hb[:, :, TT:TT + 112], 0.0, ["Hpad"])

        def xres(t0, n):
            return ["X%d" % i for i, (a, m) in enumerate(tiles) if a < t0 + n and t0 < a + m]

        def hres(t0, n):
            return ["H%d" % i for i, (a, m) in enumerate(tiles) if a < t0 + n and t0 < a + m]

        with contextlib.ExitStack() as lst:
            XT = sb("XT", [128, 2, D], F32, lst)
            srcs = [(meta_d.ap(), NMETA, PAD)]
            for i in range(S_LEN // 128):
                srcs.append((xp_d.ap()[i * 128:(i + 1) * 128, :], 128, PAD + NMETA + i * 128))
            srcs.append((xs_d.ap(), NS, TP))
            for i, (src, n, t0) in enumerate(srcs):
                sl = i % 2
                dma("sp", XT[0:n, sl, :], src, [], ["XT%d" % sl], "XT%d" % sl)
                for half in range(2):
                    bank = PS[(i * 2 + half) % 4]
                    bn = "B%d" % ((i * 2 + half) % 4)
                    for c4 in range(4):
                        c = half * 4 + c4
                        tr(bank[:, c4 * 128:c4 * 128 + n], XT[0:n, sl, c * 128:(c + 1) * 128], CSTf[0:n, 0:n],
                           ["XT%d" % sl, "CSTf"], [bn])
                    src_ap = bank[:, :].rearrange("p (a b) -> p a b", a=4)[:, :, 0:n]
                    cp("act" if half == 0 else "dve", X[:, half * 4:half * 4 + 4, t0:t0 + n], src_ap,
                       [bn], xres(t0, n))
        S.barrier()

        def rmsnorm_to_h(pname, pidx, lst):
            SQ = sb("SQ", [128, 1, KD, 512], BF16, lst)
            RS = sb("RS", [128, 2, 512], F32, lst)
            for ti, (t0, n) in enumerate(tiles):
                sl = ti % 2
                act(SQ[:, 0, :, 0:n], X[:, :, t0:t0 + n], AF.Square, ["X%d" % ti], ["SQ0"])
                bank = PS[6 + sl]
                bn = "B%d" % (6 + sl)
                mm(bank[:, 0:n], [(cb("ones"), SQ[:, 0, c, 0:n]) for c in range(KD)], ["SQ0", "CSTb"], [bn])
                act(RS[:, sl, 0:n], bank[:, 0:n], AF.Ln, [bn], ["RS%d" % sl], bias=RMS_EPS, scale=1.0 / D)
                act(RS[:, sl, 0:n], RS[:, sl, 0:n], AF.Exp, ["RS%d" % sl], ["RS%d" % sl], scale=-0.5)
                for c in range(KD):
                    stt(Hb[:, c, t0:t0 + n], X[:, c, t0:t0 + n], ppc(pname, pidx, c, 1), RS[:, sl, 0:n],
                        ALU.mult, ALU.mult, ["X%d" % ti, "RS%d" % sl, "PP"], ["H%d" % ti])

        def ffn(l, which, lst):
            gate_d = W["ffn%d_gate" % which].ap()[l].rearrange("(k p) f -> p k f", p=128)
            up_d = W["ffn%d_up" % which].ap()[l].rearrange("(k p) f -> p k f", p=128)
            down_d = W["ffn%d_down" % which].ap()[l]
            SIL = sb("SIL", [128, 2, 512], F32, lst)
            G = sb("G", [128, 2, 2, 512], BF16, lst)
            NG = DFF // 256
            views = {}

            def load(g):
                sl = g % 2
                f0 = g * 256
                wn = "FW%d" % sl
                gv = FW[:, sl, 0:2048].rearrange("p (k f) -> p k f", k=8)
                uv = FW[:, sl, 2048:4096].rearrange("p (k f) -> p k f", k=8)
                dv = FW[:, sl, 4096:6144].rearrange("p (j m) -> p j m", j=2)
                dma("pool", gv, gate_d[:, :, f0:f0 + 256], [], [wn], wn)
                dma("pool", uv, up_d[:, :, f0:f0 + 256], [], [wn], wn)
                dma("pool", dv, down_d[f0:f0 + 256, :].rearrange("(j p) m -> p j m", p=128), [], [wn], wn)
                views[g] = (gv, uv, dv, wn)

            def gate_up(it):
                g, ti, gs = it
                gv, uv, dv, wn = views[g]
                t0, n = tiles[ti]
                for j in range(2):
                    bg, bu = PS[j], PS[2 + j]
                    mm(bg[:, 0:n], [(gv[:, k, j * 128:(j + 1) * 128], Hb[:, k, t0:t0 + n]) for k in range(KD)],
                       [wn, "H%d" % ti], ["B%d" % j])
                    mm(bu[:, 0:n], [(uv[:, k, j * 128:(j + 1) * 128], Hb[:, k, t0:t0 + n]) for k in range(KD)],
                       [wn, "H%d" % ti], ["B%d" % (2 + j)])
                    act(SIL[:, j, 0:n], bg[:, 0:n], AF.Silu, ["B%d" % j], ["SIL%d" % j])
                    tt(G[:, gs, j, 0:n], SIL[:, j, 0:n], bu[:, 0:n], ALU.mult, ["SIL%d" % j, "B%d" % (2 + j)],
                       ["G%d_%d" % (gs, j)])

            def down(it):
                g, ti, gs = it
                gv, uv, dv, wn = views[g]
                t0, n = tiles[ti]
                for dm in range(KD):
                    bd = PS[4 + dm % 4]
                    bn = "B%d" % (4 + dm % 4)
                    mm(bd[:, 0:n], [(dv[:, j, dm * 128:(dm + 1) * 128], G[:, gs, j, 0:n]) for j in range(2)],
                       [wn, "G%d_0" % gs, "G%d_1" % gs], [bn])
                    stt(X[:, dm, t0:t0 + n], bd[:, 0:n], 0.5, X[:, dm, t0:t0 + n], ALU.mult, ALU.add,
                        [bn, "X%d" % ti], ["X%d" % ti])

            items = []
            for g in range(NG):
                for ti in range(NT):
                    items.append((g, ti, len(items) % 2))
            loaded = set()
            for i in range(len(items) + 1):
                if i < len(items):
                    g = items[i][0]
                    if g not in loaded:
                        load(g)
                        loaded.add(g)
                    gate_up(items[i])
                if i >= 1:
                    down(items[i - 1])

        def mcol(t0):
            for ti, (a, m) in enumerate(tiles):
                if a <= t0 < a + m:
                    return ti, t0 - a
            raise ValueError

        def outproj_load(wrows, nk):
            for part in range(nk // 4):
                wn = "FW%d" % part
                wv = FW[:, part, 0:4096].rearrange("p (j m) -> p j m", j=4)
                dma("pool", wv, wrows[part * 512:(part + 1) * 512, :].rearrange("(j p) m -> p j m", p=128), [], [wn], wn)

        def outproj(wrows, nk, ti, mix_c0, preloaded=False):
            t0, n = tiles[ti]
            if not preloaded:
                outproj_load(wrows, nk)
            for dm in range(KD):
                bank = PS[dm % 2]
                bn = "B%d" % (dm % 2)
                pairs = []
                for j in range(nk):
                    wv = FW[:, j // 4, 0:4096].rearrange("p (j m) -> p j m", j=4)
                    pairs.append((wv[:, j % 4, dm * 128:(dm + 1) * 128], MIXT[:, mix_c0 + j, 0:n]))
                mm(bank[:, 0:n], pairs, ["FW0", "FW1", "MIXT"], [bn])
                stt(X[:, dm, t0:t0 + n], bank[:, 0:n], 1.0, X[:, dm, t0:t0 + n], ALU.mult, ALU.add,
                    [bn, "X%d" % ti], ["X%d" % ti])

        preloaded_mw = set()

        def gla_load(kind, idx):
            is_h = kind == "hgrn"
            ncol = 2048 if is_h else 2064
            win = (W["even_w_in"] if is_h else W["odd_w_in"]).ap()[idx].rearrange("(k p) f -> p k f", p=128)
            WA = MW[:, 0:8 * ncol].rearrange("p (k f) -> p k f", k=8)
            for c0 in range(0, ncol, 512):
                c1 = min(ncol, c0 + 512)
                dma("pool", WA[:, :, c0:c1], win[:, :, c0:c1], [], ["MW"], "MW")
            if not is_h:
                dma("pool", MW[0:16, 16512:17024], W["gla_gate_up"].ap()[idx], [], ["MW"], "MW")
            preloaded_mw.add((kind, idx))

        def gla_pass(kind, idx, lst):
            is_h = kind == "hgrn"
            dv = 128 if is_h else 256
            NV = 4 * dv
            nvc = dv // 128
            esc = 1.0 if is_h else -1.0 / 16.0
            ncol = 2048 if is_h else 2064
            WA = MW[:, 0:8 * ncol].rearrange("p (k f) -> p k f", k=8)
            GU = MW[0:16, 16512:17024]
            if (kind, idx) not in preloaded_mw:
                gla_load(kind, idx)
            WK = sb("WK", [128, 10, 256], F32, lst)
            QTb = sb("QTb", [128, 4, 64], BF16, lst)
            KTb = sb("KTb", [128, 4, 128], BF16, lst)
            KDb = sb("KDb", [128, 4, 128], BF16, lst)
            Vb = sb("Vb", [128, NV], BF16, lst)
            KDT = sb("KDT", [128, 512], BF16, lst)
            ATb = sb("ATb", [128, 4, 64], BF16, lst)
            TV = sb("TV", [128, 512], F32, lst)
            SQb = sb("SQb", [128, 4 * nvc, 64], BF16, lst)
            RSn = sb("RSn", [128, 4, 64], F32, lst)
            ST = sb("ST", [128, 4, dv], F32, lst)
            STb = sb("STb", [128, 4, dv], BF16, lst)
            GDb = sb("GDb", [128, 64], BF16, lst)
            LB = sb("LB", [128, 8], F32, lst)
            SS = sb("SS", [128, 1, 4, dv], F32, lst)
            QS = sb("QS", [128, 4, 16], F32, lst)
            KS = sb("KS", [128, 4, 16], F32, lst)
            memset("dve", ST[:], 0.0, ["ST"])
            memset("dve", STb[:], 0.0, ["STb"])
            memset("dve", KTb[:], 0.0, ["KTb"])
            memset("dve", KDb[:], 0.0, ["KDb"])
            if is_h:
                if idx == 0:
                    memset("dve", LB[:, 0:4], 0.0, ["LB"])
                else:
                    assert NE == 2
                    tt(LB[:, 4:8], ppc("hgrn_lb", 0), ppc("hgrn_lb", 1), ALU.subtract, ["PP"], ["LBt"])
                    sigmoid_chain(LB[:, 0:4], LB[:, 4:8], LB[:, 4:8], ["LBt"], ["LB"], "LBt", scale_in=-1.0)
                ts(LB[:, 4:8], LB[:, 0:4], -1.0, 1.0, ALU.mult, ALU.add, ["LB"], ["OML"])

            def v3(ap2, n, nh=4):
                return ap2.rearrange("p (h t) -> p h t", h=nh)[:, :, 0:n]

            def wk(i, n, nh=4):
                return WK[:, i, :].rearrange("p (h t) -> p h t", h=nh)[:, :, 0:n]

            def prep_proj(t0, n):
                tok = slice(t0, t0 + n)
                hr = hres(t0, n)
                q_ps = v3(PS[0][:, 0:256], n)
                f_ps = v3(PS[0][:, 256:512], n)
                g_ps = v3(PS[1][:, 0:256], n)
                for h in range(4):
                    mm(q_ps[:, h, :], [(WA[:, k, h * 128:(h + 1) * 128], Hb[:, k, tok]) for k in range(KD)],
                       ["MW"] + hr, ["B0"])
                for h in range(4):
                    mm(f_ps[:, h, :], [(WA[:, k, 512 + h * 128:512 + (h + 1) * 128], Hb[:, k, tok]) for k in range(KD)],
                       ["MW"] + hr, ["B0"])
                if is_h:
                    for h in range(4):
                        mm(g_ps[:, h, :], [(WA[:, k, 1536 + h * 128:1536 + (h + 1) * 128], Hb[:, k, tok])
                                           for k in range(KD)], ["MW"] + hr, ["B1"])
                else:
                    gd_ps = PS[1][0:16, 256:256 + n]
                    mm(gd_ps, [(WA[:, k, 2048:2064], Hb[:, k, tok]) for k in range(KD)], ["MW"] + hr, ["B1"])
                    cp("act", GDb[0:16, 0:n], gd_ps, ["B1"], ["GDb"])
                    for h in range(4):
                        mm(g_ps[:, h, :], [(GU[:, h * 128:(h + 1) * 128], GDb[0:16, 0:n])], ["MW", "GDb"], ["B1"])
                for j in range(NV // 512):
                    mm(PS[2 + j][:, :], [(Hb[:, k, t0:t0 + 128], WA[:, k, 1024 + j * 512:1024 + (j + 1) * 512])
                                         for k in range(KD)], ["MW", "Hpad"] + hres(t0, 128), ["B%d" % (2 + j)])

            def prep(t0, n, sample):
                q_ps = v3(PS[0][:, 0:256], n)
                f_ps = v3(PS[0][:, 256:512], n)
                g_ps = v3(PS[1][:, 0:256], n)
                vr = n if sample else 128
                if is_h:
                    sigmoid_chain(wk(1, n), f_ps, wk(0, n), ["B0"], ["T1"], "T0")
                    act(TV[0:vr, 0:NV], PS[2][0:vr, :], AF.Exp, ["B2"], ["TV"], scale=-1.0)
                    act(TV[0:vr, 0:NV], TV[0:vr, 0:NV], AF.Ln, ["TV"], ["TV"], bias=1.0)
                    act(TV[0:vr, 0:NV], TV[0:vr, 0:NV], AF.Exp, ["TV"], ["TV"], scale=-1.0)
                    tt(wk(2, n), wk(1, n), LB[:, 4:8].unsqueeze(2).to_broadcast([128, 4, n]), ALU.mult, ["T1", "OML"], ["T2"])
                    tt(wk(2, n), wk(2, n), LB[:, 0:4].unsqueeze(2).to_broadcast([128, 4, n]), ALU.add, ["T2", "LB"], ["T2"])
                    ts(wk(3, n), wk(2, n), -1.0, 1.0, ALU.mult, ALU.add, ["T2"], ["T3"])
                    act(wk(4, n), wk(2, n), AF.Ln, ["T2"], ["T4"])
                    tt(Vb[0:vr, 0:NV], PS[2][0:vr, :], TV[0:vr, 0:NV], ALU.mult, ["B2", "TV"], ["Vb"])
                    sigmoid_chain(wk(1, n), g_ps, wk(0, n), ["B1"], ["T1"], "T0")
                    kf, kfr = wk(3, n), ["T3"]
                else:
                    gb = ppc("gla_gate_b", idx).unsqueeze(2).to_broadcast([128, 4, n])
                    tt(wk(2, n), g_ps, gb, ALU.add, ["B1", "PP"], ["T2"])
                    act(wk(0, n), wk(2, n), AF.Exp, ["T2"], ["T0"], scale=-1.0)
                    act(wk(4, n), wk(0, n), AF.Ln, ["T0"], ["T4"], bias=1.0)
                    for j in range(2):
                        cp("act", Vb[0:vr, j * 512:(j + 1) * 512], PS[2 + j][0:vr, :], ["B%d" % (2 + j)], ["Vb"])
                    kf, kfr = f_ps, ["B0"]
                if not sample:
                    for h in range(4):
                        S.op("dve", lambda e, h=h: e.tensor_tensor_scan(out=wk(5, n)[:, h, :], data0=ONESf[:, 0:n],
                                                                      data1=wk(4, n)[:, h, :], initial=0.0,
                                                                      op0=ALU.mult, op1=ALU.add),
                             ["T4", "ONESf"], ["T5"])
                    gc, gcr = wk(5, n), ["T5"]
                else:
                    gc, gcr = wk(4, n), ["T4"]
                act(wk(6, n), gc, AF.Exp, gcr, ["T6"], scale=esc)
                if sample:
                    ts(QS[:, :, 0:n], q_ps, 128.0 ** -0.5, None, ALU.mult, None, ["B0"], ["QS"])
                    cp("dve", KS[:, :, 0:n], kf, kfr, ["KS"])
                    return
                act(wk(7, n), gc, AF.Exp, gcr, ["T7"], scale=-esc)
                stt(QTb[:, :, 0:n], q_ps, 128.0 ** -0.5, wk(6, n), ALU.mult, ALU.mult, ["B0", "T6"], ["QTb"])
                tt(KTb[:, :, 0:n], wk(7, n), kf, ALU.mult, ["T7"] + kfr, ["KTb"])
                for h in range(4):
                    stt(KDb[:, h, 0:n], wk(7, n)[:, h, :], wk(6, n)[:, h, n - 1:n], kf[:, h, :], ALU.mult, ALU.mult,
                        ["T7", "T6"] + kfr, ["KDb"])

            def recur(n):
                sc_ps = v3(PS[4][:, 0:256], n)
                for h in range(4):
                    mm(sc_ps[:, h, :], [(KTb[:, h, :], QTb[:, h, 0:n])], ["KTb", "QTb"], ["B4a"])
                for h in range(4):
                    tr(PSb[4][:, 512 + h * 128:512 + (h + 1) * 128], KDb[:, h, :], cb("ident"), ["KDb", "CSTb"], ["B4b"])
                cp("act", wk(8, n), sc_ps, ["B4a"], ["T8"])
                cp("act", KDT[:, :], PSb[4][:, 512:1024], ["B4b"], ["KDT"])
                tt(ATb[:, :, 0:n], wk(8, n), cf("causT")[:, 0:n].unsqueeze(1).to_broadcast([128, 4, n]), ALU.mult,
                   ["T8", "CSTf"], ["ATb"])
                if DEBUG_STAGE < 1.7:
                    return
                o_ps = v3(PS[5][:, 0:4 * nvc * 64], n, 4 * nvc)
                for h in range(4):
                    for vc in range(nvc):
                        c0 = h * dv + vc * 128
                        mm(o_ps[:, h * nvc + vc, :], [(Vb[:, c0:c0 + 128], ATb[:, h, 0:n]),
                                                      (STb[:, h, vc * 128:(vc + 1) * 128], QTb[:, h, 0:n])],
                           ["Vb", "ATb", "STb", "QTb"], ["B5"])
                if DEBUG_STAGE < 1.9:
                    return
                for h in range(4):
                    bank = PS[6 + (h * dv) // 512]
                    c0 = (h * dv) % 512
                    mm(bank[:, c0:c0 + dv], [(KDT[:, h * 128:(h + 1) * 128], Vb[:, h * dv:(h + 1) * dv])],
                       ["KDT", "Vb"], ["B%d" % (6 + (h * dv) // 512)])
                for h in range(4):
                    bank = PS[6 + (h * dv) // 512]
                    c0 = (h * dv) % 512
                    stt(ST[:, h, :], ST[:, h, :], wk(6, n)[:, h, n - 1:n], bank[:, c0:c0 + dv], ALU.mult, ALU.add,
                        ["T6", "B%d" % (6 + (h * dv) // 512), "ST"], ["ST"])
                cp("act", STb[:], ST[:], ["ST"], ["STb"])

            def post(t0, n):
                ti, mc = mcol(t0)
                o_ps = v3(PS[5][:, 0:4 * nvc * 64], n, 4 * nvc)
                g_ps = v3(PS[1][:, 0:256], n)
                if is_h:
                    tt(wk(9, n), o_ps, wk(1, n), ALU.mult, ["B5", "T1"], ["T9"])
                    act(SQb[:, 0:4, 0:n], wk(9, n), AF.Square, ["T9"], ["SQb"])
                    mm(PS[7][:, 0:n], [(cb("ones"), SQb[:, h, 0:n]) for h in range(4)], ["SQb", "CSTb"], ["B7"])
                    act(RSn[:, 0, 0:n], PS[7][:, 0:n], AF.Ln, ["B7"], ["RSn"], bias=RMS_EPS, scale=1.0 / 512)
                    act(RSn[:, 0, 0:n], RSn[:, 0, 0:n], AF.Exp, ["RSn"], ["RSn"], scale=-0.5)
                    tt(wk(9, n), wk(9, n), RSn[:, 0:1, 0:n].to_broadcast([128, 4, n]), ALU.mult, ["T9", "RSn"], ["T9"])
                    tt(MIXT[:, 0:4, mc:mc + n], wk(9, n), ppc("hgrn_norm", idx).unsqueeze(2).to_broadcast([128, 4, n]),
                       ALU.mult, ["T9", "PP"], ["MIXT"])
                else:
                    act(SQb[:, :, 0:n], o_ps, AF.Square, ["B5"], ["SQb"])
                    s_ps = v3(PS[7][:, 0:256], n)
                    for h in range(4):
                        mm(s_ps[:, h, :], [(cb("ones"), SQb[:, h * 2 + vc, 0:n]) for vc in range(2)],
                           ["SQb", "CSTb"], ["B7"])
                    act(RSn[:, :, 0:n], s_ps, AF.Ln, ["B7"], ["RSn"], bias=RMS_EPS, scale=1.0 / 256)
                    act(RSn[:, :, 0:n], RSn[:, :, 0:n], AF.Exp, ["RSn"], ["RSn"], scale=-0.5)
                    o4 = o_ps.rearrange("p (h v) t -> p h v t", v=2)
                    m4 = MIXT[:, 0:8, mc:mc + n].rearrange("p (h v) t -> p h v t", v=2)
                    for vc in range(2):
                        stt(m4[:, :, vc, :], o4[:, :, vc, :], ppc("gla_norm", idx, vc, 1), RSn[:, :, 0:n],
                            ALU.mult, ALU.mult, ["B5", "RSn", "PP"], ["MIXT"])

            def sample_update():
                n = NS
                st_in = (sth_d if is_h else stg_d).ap()[idx]
                st_out = (hs_d if is_h else gs_d).ap()[idx]
                o_ps = v3(PS[5][:, 0:4 * nvc * 64], n, 4 * nvc)
                for b in range(NS):
                    selb = cb("sel", rows=NS)[:, b * 128:(b + 1) * 128]
                    if is_h:
                        vb = PS[6 + b % 2]
                        vbn = "B%d" % (6 + b % 2)
                        mm(vb[:, :], [(selb, Vb[0:NS, 0:512])], ["CSTb", "Vb"], [vbn])
                    else:
                        for j in range(2):
                            mm(PS[6 + j][:, :], [(selb, Vb[0:NS, j * 512:(j + 1) * 512])], ["CSTb", "Vb"], ["B%d" % (6 + j)])
                    for hh in range(2):
                        sn = "SS%d" % hh
                        hs = slice(2 * hh, 2 * hh + 2)
                        ssv = SS[:, 0, hs, :]
                        dma("sp", ssv, st_in[b][2 * hh:2 * hh + 2].rearrange("h k v -> k h v"), [], [sn], sn)
                        if is_h:
                            outv = WK[:, 7 + hh, :].rearrange("p (h v) -> p h v", h=2)
                            on = ["T%d" % (7 + hh)]
                        else:
                            outv = WK[:, 7:9, :] if hh == 0 else WK[:, 0:2, :]
                            on = ["T7", "T8"] if hh == 0 else ["T0", "T1"]
                        for h in (2 * hh, 2 * hh + 1):
                            if is_h:
                                vsrc = vb[:, h * 128:(h + 1) * 128]
                                vres = vbn
                            else:
                                vsrc = PS[6 + hh][:, (h % 2) * 256:(h % 2 + 1) * 256]
                                vres = "B%d" % (6 + hh)
                            ts(SS[:, 0, h, :], SS[:, 0, h, :], wk(6, n)[:, h, b:b + 1], None, ALU.mult, None, [sn, "T6"], [sn])
                            stt(outv[:, h % 2, :], vsrc, KS[:, h, b:b + 1], SS[:, 0, h, :], ALU.mult, ALU.add,
                                [vres, "KS", sn], on)
                        dma("pool", st_out[b][2 * hh:2 * hh + 2].rearrange("h k v -> k h v"), outv, on, [], "SSo%d" % hh)
                        for h in (2 * hh, 2 * hh + 1):
                            for vc in range(nvc):
                                mm(o_ps[:, h * nvc + vc, b:b + 1], [(outv[:, h % 2, vc * 128:(vc + 1) * 128], QS[:, h, b:b + 1])],
                                   on + ["QS"], ["B5"])

            def tile_prefetch():
                if is_h:
                    outproj_load(W["even_w_out"].ap()[idx][0:512, :], 4)
                else:
                    og_d = W["odd_w_in"].ap()[idx].rearrange("(k p) f -> p k f", p=128)
                    for half in range(2):
                        wn = "FW%d" % half
                        wv = FW[:, half, 0:4096].rearrange("p (k f) -> p k f", k=8)
                        dma("pool", wv, og_d[:, :, 2064 + half * 512:2064 + (half + 1) * 512], [], [wn], wn)

            def finish_tile(ti):
                t0, n = tiles[ti]
                if is_h:
                    outproj(W["even_w_out"].ap()[idx][0:512, :], 4, ti, 0, preloaded=True)
                else:
                    for half in range(2):
                        wn = "FW%d" % half
                        wv = FW[:, half, 0:4096].rearrange("p (k f) -> p k f", k=8)
                        for c4 in range(4):
                            bank = PS[c4 % 2]
                            bn = "B%d" % (c4 % 2)
                            mm(bank[:, 0:n], [(wv[:, k, c4 * 128:(c4 + 1) * 128], Hb[:, k, t0:t0 + n]) for k in range(KD)],
                               [wn, "H%d" % ti], [bn])
                            sg = TV[:, 0:512]
                            act(sg[:, 0:n], bank[:, 0:n], AF.Silu, [bn], ["TV"])
                            tt(MIXT[:, half * 4 + c4, 0:n], MIXT[:, half * 4 + c4, 0:n], sg[:, 0:n], ALU.mult,
                               ["TV", "MIXT"], ["MIXT"])
                        wo = FW[:, half, 0:4096].rearrange("p (j m) -> p j m", j=4)
                        dma("pool", wo, W["odd_w_out"].ap()[idx][half * 512:(half + 1) * 512, :].rearrange("(j p) m -> p j m", p=128),
                            [], [wn], wn)
                    outproj(W["odd_w_out"].ap()[idx], 8, ti, 0, preloaded=True)

            seq = []
            for ti, (t0, n) in enumerate(tiles):
                seq.append(("w", ti, 0))
                c = t0
                while c + CH <= min(t0 + n, TP):
                    seq.append(("c", c, CH))
                    c += CH
                if t0 + n > TP:
                    seq.append(("s", TP, NS))
                seq.append(("f", ti, 0))
            done_proj = set()
            for k, ev in enumerate(seq):
                if ev[0] == "w":
                    tile_prefetch()
                    continue
                if ev[0] == "f":
                    finish_tile(ev[1])
                    continue
                _, t0c, ncc = ev
                smp = ev[0] == "s"
                if k not in done_proj:
                    prep_proj(t0c, ncc)
                prep(t0c, ncc, smp)
                if smp:
                    sample_update()
                else:
                    recur(CH)
                nxt = seq[k + 1] if k + 1 < len(seq) else None
                if nxt is not None and nxt[0] in ("c", "s"):
                    prep_proj(nxt[1], nxt[2])
                    done_proj.add(k + 1)
                post(t0c, ncc)
            pst = (hp_d if is_h else gp_d).ap()[idx]
            dma("sp", pst.rearrange("h k v -> k h v"), ST[:], ["ST"], [], "STo")


        def rwkv_pass(idx, lst):
            C0 = float(np.exp(-0.5))
            win = W["even_w_in"].ap()[idx].rearrange("(k p) f -> p k f", p=128)
            WB = MW[:, 0:14336].rearrange("p (k f) -> p k f", k=8)
            for c0 in range(0, 1792, 512):
                c1 = min(1792, c0 + 512)
                dma("pool", WB[:, :, c0:c1], win[:, :, 2048 + c0:2048 + c1], [], ["MW"], "MW")
            LW = MW[:, 14336:14848]
            G2 = MW[:, 14848:15360]
            dma("pool", LW[0:64, :], W["rwkv_w2"].ap()[idx], [], ["MW"], "MW")
            dma("pool", LW[64:128, :], W["rwkv_a2"].ap()[idx], [], ["MW"], "MW")
            dma("pool", G2, W["rwkv_g2"].ap()[idx], [], ["MW"], "MW")
            FWf = FW.bitcast(F32)
            PB = sb("PB", [128, 14, 65], F32, lst)
            XM = sb("XM", [128, 14, 64], F32, lst)
            PRV = sb("PRV", [128, 14, 16], F32, lst)
            GC = sb("GC", [128, 4], F32, lst)
            THp = sb("THp", [128, 64], BF16, lst)
            XAp = sb("XAp", [128, 64], BF16, lst)
            SGb = sb("SGb", [128, 64], BF16, lst)
            SQb = sb("SQbr", [128, 4, 64], BF16, lst)
            P = sb("P", [128, 4, 128], F32, lst)
            Pb = sb("Pb", [128, 4, 128], BF16, lst)
            memset("dve", PB[:], 0.0, ["PB"])
            memset("dve", THp[:], 0.0, ["THp"])
            memset("dve", XAp[:], 0.0, ["XAp"])
            memset("dve", P[:], 0.0, ["P"])
            memset("dve", Pb[:], 0.0, ["Pb"])
            mu = ppc("rwkv_mu", idx)

            def F(i, n):
                return FWf[:, 1, i * 256:(i + 1) * 256].rearrange("p (h t) -> p h t", h=4)[:, :, 0:n]

            def Fn(i):
                return "F%d" % i

            def v4(ap2, n):
                return ap2.rearrange("p (h t) -> p h t", h=4)[:, :, 0:n]

            def bc(name, n, c0=0, nh=4):
                return ppc(name, idx, c0, nh).unsqueeze(2).to_broadcast([128, nh, n])

            def prep_proj(t0, n):
                tok = slice(t0, t0 + n)
                hr = hres(t0, n)
                p0 = PS[0][:, :].rearrange("p (c t) -> p c t", c=8)[:, :, 0:n]
                p1 = PS[1][:, 0:384].rearrange("p (c t) -> p c t", c=6)[:, :, 0:n]
                for c in range(14):
                    dst = p0[:, c, :] if c < 8 else p1[:, c - 8, :]
                    mm(dst, [(WB[:, k, c * 128:(c + 1) * 128], Hb[:, k, tok]) for k in range(KD)], ["MW"] + hr,
                       ["B0" if c < 8 else "B1"])

            def prep(t0, n, sample):
                p0 = PS[0][:, :].rearrange("p (c t) -> p c t", c=8)[:, :, 0:n]
                p1 = PS[1][:, 0:384].rearrange("p (c t) -> p c t", c=6)[:, :, 0:n]
                cp("act", PB[:, 8:14, 1:1 + n], p1, ["B1"], ["PB"])
                cp("act", PB[:, 0:8, 1:1 + n], p0, ["B0"], ["PB"])
                cur = PB[:, :, 1:1 + n]
                prev = PRV[:, :, 0:n] if sample else PB[:, :, 0:n]
                xm = XM[:, :, 0:n]
                tt(xm, prev, cur, ALU.subtract, ["PB", "PRV"], ["XM"])
                tt(xm, xm, mu.unsqueeze(2).to_broadcast([128, 14, n]), ALU.mult, ["XM", "PP"], ["XM"])
                tt(xm, xm, cur, ALU.add, ["XM", "PB"], ["XM"])
                if not sample:
                    cp("dve", PB[:, :, 0:1], PB[:, :, n:n + 1], ["PB", "XM"], ["PB"])
                xr, xk, xv = XM[:, 0:4, 0:n], XM[:, 4:8, 0:n], XM[:, 8:12, 0:n]
                sigmoid_chain(F(0, n)[0:64, 0, :], XM[0:64, 12, 0:n], F(0, n)[0:64, 1, :], ["XM"], [Fn(0)], Fn(0), scale_in=2.0)
                cp("dve", XAp[64:128, 0:n], XM[64:128, 12, 0:n], ["XM"], ["XAp"])
                tt(F(8, n), xk, bc("rwkv_kk", n), ALU.mult, ["XM", "PP"], [Fn(8)])
                tt(SQb[:, :, 0:n], F(8, n), F(8, n), ALU.mult, [Fn(8)], ["SQb"])
                wv_ps = v4(PS[2][:, 0:256], n)
                a_ps = v4(PS[2][:, 256:512], n)
                g_ps = v4(PS[3][:, 0:256], n)
                ss_ps = v4(PS[3][:, 256:512], n)
                for c in range(4):
                    mm(a_ps[:, c, :], [(LW[:, c * 128:(c + 1) * 128], XAp[:, 0:n])], ["MW", "XAp"], ["B2"])
                for c in range(4):
                    mm(ss_ps[:, c, :], [(cb("blk"), SQb[:, c, 0:n])], ["CSTb", "SQb"], ["B3"])
                ts(THp[0:64, 0:n], F(0, n)[0:64, 0, :], 2.0, -1.0, ALU.mult, ALU.add, [Fn(0)], ["THp"])
                for c in range(4):
                    mm(wv_ps[:, c, :], [(LW[:, c * 128:(c + 1) * 128], THp[:, 0:n])], ["MW", "THp"], ["B2"])
                sigmoid_chain(F(0, n)[:, 3, :], XM[:, 13, 0:n], F(0, n)[:, 2, :], ["XM"], ["F0g"], "F0g")
                cp("act", SGb[:, 0:n], F(0, n)[:, 3, :], ["F0g"], ["SGb"])
                for c in range(4):
                    mm(g_ps[:, c, :], [(G2[:, c * 128:(c + 1) * 128], SGb[:, 0:n])], ["MW", "SGb"], ["B3"])
                tt(F(2, n), a_ps, bc("rwkv_a0", n), ALU.add, ["B2", "PP"], [Fn(2)])
                tt(F(1, n), wv_ps, bc("rwkv_w0", n), ALU.add, ["B2", "PP"], [Fn(1)])
                sigmoid_chain(F(1, n), F(1, n), F(3, n), [Fn(1)], [Fn(1)], Fn(3))
                act(F(9, n), ss_ps, AF.Ln, ["B3"], [Fn(9)], bias=1e-18)
                act(F(9, n), F(9, n), AF.Exp, [Fn(9)], [Fn(9)], scale=-0.5)
                sigmoid_chain(F(2, n), F(2, n), F(10, n), [Fn(2)], [Fn(2)], Fn(10))
                tt(F(8, n), F(8, n), F(9, n), ALU.mult, [Fn(8), Fn(9)], [Fn(8)])
                if sample:
                    act(F(5, n), F(1, n), AF.Exp, [Fn(1)], [Fn(5)], scale=-C0)
                else:
                    for c in range(4):
                        S.op("dve", lambda e, c=c: e.tensor_tensor_scan(out=F(3, n)[:, c, :], data0=ONESf[:, 0:n],
                                                                      data1=F(1, n)[:, c, :], initial=0.0,
                                                                      op0=ALU.mult, op1=ALU.add),
                             [Fn(1), "ONESf"], [Fn(3)])
                    tt(F(4, n), F(3, n), F(1, n), ALU.subtract, [Fn(3), Fn(1)], [Fn(4)])
                    act(F(5, n), F(3, n), AF.Exp, [Fn(3)], [Fn(5)], scale=-C0)
                    act(F(6, n), F(3, n), AF.Exp, [Fn(3)], [Fn(6)], scale=C0)
                    act(F(4, n), F(4, n), AF.Exp, [Fn(4)], [Fn(4)], scale=-C0)
                cp("act", F(11, n), g_ps, ["B3"], [Fn(11)])
                for c in range(4):
                    ts(F(9, n)[:, c, :], F(2, n)[:, c, :], -1.0, ppc("rwkv_ka", idx, c, 1), ALU.add, ALU.mult,
                       [Fn(2), "PP"], [Fn(9)])
                stt(F(9, n), F(9, n), 1.0, xk, ALU.add, ALU.mult, [Fn(9), "XM"], [Fn(9)])
                tt(F(10, n), F(8, n), F(2, n), ALU.mult, [Fn(8), Fn(2)], [Fn(10)])
                tt(F(0, n), xr, F(9, n), ALU.mult, ["XM", Fn(9)], [Fn(0), "F0g"])
                tt(SQb[:, :, 0:n], F(0, n), bc("rwkv_rk", n), ALU.mult, [Fn(0), "PP"], ["SQb"])
                bs_ps = v4(PS[4][:, 0:256], n)
                for c in range(4):
                    mm(bs_ps[:, c, :], [(cb("blk"), SQb[:, c, 0:n])], ["CSTb", "SQb"], ["B4"])
                if not sample:
                    cp("dve", GC[:, :], F(5, n)[:, :, n - 1], [Fn(5)], ["GC"])
                tt(F(7, n), bs_ps, xv, ALU.mult, ["B4", "XM"], [Fn(7)])

            with contextlib.ExitStack() as pst:
                AR = sb("ARbd", [128, 4, 2, 128], BF16, pst)
                Bd = sb("Bbd", [128, 4, 128], BF16, pst)
                Kd = sb("Kbd", [128, 4, 128], BF16, pst)
                Vd = sb("Vbd", [128, 4, 128], BF16, pst)
                BT = sb("BTbd", [128, 4, 128], BF16, pst)
                KT = sb("KTbd", [128, 4, 128], BF16, pst)
                VT = sb("VTbd", [128, 4, 128], BF16, pst)
                NK = sb("NK", [128, 4, 2, 128], BF16, pst)
                NTK = sb("NTK", [128, 4, 2, 256], BF16, pst)
                Wb = sb("Wb", [128, 4, 128], BF16, pst)
                Ub = sb("Ub", [128, 4, 128], BF16, pst)
                YFb = sb("YFb", [128, 4, 64], BF16, pst)
                for t_ in (AR, Bd, Kd, Vd):
                    memset("dve", t_[:], 0.0, [])
                S.barrier()
                SCA = MIXT[:, 0:2, :].rearrange("p a (b c) -> p (a b) c", b=2)
                SCB = MIXT[:, 2:4, :].rearrange("p a (b c) -> p (a b) c", b=2)
                SCa = [SCA[:, c, :] for c in range(4)]
                SCb = [SCB[:, c, :] for c in range(4)]

                def chunk(t0):
                    n = CH
                    prep(t0, n, False)
                    xr, xv = XM[:, 0:4, 0:n], XM[:, 8:12, 0:n]
                    for hf in range(2):
                        ps_ = slice(hf * 64, hf * 64 + 64)
                        cs_ = slice(hf * 64, hf * 64 + 64)
                        stt(AR[ps_, :, 0, cs_], F(8, n)[ps_], -1.0, F(4, n)[ps_], ALU.mult, ALU.mult, [Fn(8), Fn(4)], ["AR"])
                        tt(AR[ps_, :, 1, cs_], xr[ps_], F(5, n)[ps_], ALU.mult, ["XM", Fn(5)], ["AR"])
                        tt(Bd[ps_, :, cs_], F(10, n)[ps_], F(6, n)[ps_], ALU.mult, [Fn(10), Fn(6)], ["Bd"])
                        tt(Kd[ps_, :, cs_], F(9, n)[ps_], F(6, n)[ps_], ALU.mult, [Fn(9), Fn(6)], ["Kd"])
                        cp("act", Vd[ps_, :, cs_], xv[ps_], ["XM"], ["Vd"])
                    for c in range(4):
                        tr(PSb[5][:, c * 128:(c + 1) * 128], Bd[:, c, :], cb("ident"), ["Bd", "CSTb"], ["B5"])
                    for c in range(4):
                        tr(PSb[5][:, 512 + c * 128:512 + (c + 1) * 128], Kd[:, c, :], cb("ident"), ["Kd", "CSTb"], ["B5"])
                    for c in range(4):
                        tr(PSb[6][:, c * 128:(c + 1) * 128], Vd[:, c, :], cb("ident"), ["Vd", "CSTb"], ["B6"])
                    cp("act", BT[:].rearrange("p a b -> p (a b)"), PSb[5][:, 0:512], ["B5"], ["BT"])
                    cp("act", KT[:].rearrange("p a b -> p (a b)"), PSb[5][:, 512:1024], ["B5"], ["KT"])
                    cp("act", VT[:].rearrange("p a b -> p (a b)"), PSb[6][:, 0:512], ["B6"], ["VT"])
                    m2 = cf("mt2").unsqueeze(1).to_broadcast([128, 2, 256])
                    for pp_ in range(2):
                        for q_ in range(2):
                            c = 2 * pp_ + q_
                            arv = AR[:, c, :, :].rearrange("p a b -> p (a b)")
                            mm(PS[7][:, q_ * 256:(q_ + 1) * 256], [(Bd[:, c, :], arv)], ["Bd", "AR"], ["B7"])
                            mm(PS[6][:, q_ * 256:(q_ + 1) * 256], [(Kd[:, c, :], arv)], ["Kd", "AR"], ["B6"])
                        tt(SCA[:, 2 * pp_:2 * pp_ + 2, :], PS[7][:, :].rearrange("p (a b) -> p a b", a=2), m2, ALU.mult,
                           ["B7", "CSTf"], ["SCa%d" % (2 * pp_), "SCa%d" % (2 * pp_ + 1)])
                        tt(SCB[:, 2 * pp_:2 * pp_ + 2, :], PS[6][:, :].rearrange("p (a b) -> p a b", a=2), m2, ALU.mult,
                           ["B6", "CSTf"], ["SCb%d" % (2 * pp_), "SCb%d" % (2 * pp_ + 1)])
                    for c in range(4):
                        mm(PS[5][:, c * 128:(c + 1) * 128], [(AR[:, c, 0, :], Bd[:, c, :])], ["AR", "Bd"], ["B5"])
                    tt(NK[:, :, 0, :], PS[5][:, :].rearrange("p (a b) -> p a b", a=4),
                       cf("mstr").unsqueeze(1).to_broadcast([128, 4, 128]), ALU.mult, ["B5", "CSTf"],
                       ["NK%d_0" % c for c in range(4)])
                    cp("act", NTK[:, :, 0, 0:128], SCA[:, :, 0:128], ["SCa%d" % c for c in range(4)],
                       ["NTK%d_0" % c for c in range(4)])
                    cp("act", NTK[:, :, 0, 128:256], cb("ident").unsqueeze(1).to_broadcast([128, 4, 128]), ["CSTb"],
                       ["NTK%d_0" % c for c in range(4)])
                    for k in range(6):
                        pa, pn = k % 2, (k + 1) % 2
                        for c in range(4):
                            bn = "B%d" % c
                            last = k == 5
                            if last:
                                mm(PS[c][:, 128:256], [(NK[:, c, pa, :], NTK[:, c, pa, 128:256])],
                                   ["NK%d_%d" % (c, pa), "NTK%d_%d" % (c, pa)], [bn])
                            else:
                                mm(PS[c][:, 0:256], [(NK[:, c, pa, :], NTK[:, c, pa, :])],
                                   ["NK%d_%d" % (c, pa), "NTK%d_%d" % (c, pa)], [bn])
                                mm(PS[c][:, 256:384], [(NTK[:, c, pa, 0:128], NK[:, c, pa, :])],
                                   ["NK%d_%d" % (c, pa), "NTK%d_%d" % (c, pa)], [bn])
                            tt(NTK[:, c, pn, 128:256], NTK[:, c, pa, 128:256], PS[c][:, 128:256], ALU.add,
                               ["NTK%d_%d" % (c, pa), bn], ["NTK%d_%d" % (c, pn)])
                            if not last:
                                cp("act", NTK[:, c, pn, 0:128], PS[c][:, 0:128], [bn], ["NTK%d_%d" % (c, pn)])
                                cp("act", NK[:, c, pn, :], PS[c][:, 256:384], [bn], ["NK%d_%d" % (c, pn)])
                    for c in range(4):
                        TTc = NTK[:, c, 0, 128:256]
                        mm(PS[4][:, c * 128:(c + 1) * 128], [(AR[:, c, 0, :], Pb[:, c, :]), (SCb[c][:, 0:128], VT[:, c, :])],
                           ["AR", "Pb", "SCb%d" % c, "VT"], ["B4"])
                    cp("act", Wb[:].rearrange("p a b -> p (a b)"), PS[4][:, :], ["B4"], ["Wb"])
                    for c in range(4):
                        mm(PS[5][:, c * 128:(c + 1) * 128], [(NTK[:, c, 0, 128:256], Wb[:, c, :])], ["NTK%d_0" % c, "Wb"], ["B5"])
                    cp("act", Ub[:].rearrange("p a b -> p (a b)"), PS[5][:, :], ["B5"], ["Ub"])
                    for c in range(4):
                        mm(PS[6][:, c * 128:(c + 1) * 128],
                           [(Pb[:, c, :], AR[:, c, 1, :]), (Ub[:, c, :], SCa[c][:, 128:256]), (VT[:, c, :], SCb[c][:, 128:256])],
                           ["Pb", "AR", "Ub", "SCa%d" % c, "SCb%d" % c, "VT"], ["B6"])
                    for c in range(4):
                        mm(PS[7][:, c * 128:(c + 1) * 128], [(BT[:, c, :], Ub[:, c, :]), (KT[:, c, :], VT[:, c, :])],
                           ["BT", "KT", "Ub", "VT"], ["B7"])
                    yv = PS[6][:, :].rearrange("p (c t) -> p c t", c=4)
                    cp("act", F(6, n)[0:64], yv[0:64, :, 0:64], ["B6"], [Fn(6)])
                    cp("act", F(6, n)[64:128], yv[64:128, :, 64:128], ["B6"], [Fn(6)])
                    tt(P[:], P[:], PS[7][:, :].rearrange("p (a b) -> p a b", a=4), ALU.add, ["P", "B7"], ["P"])
                    tt(P[:], P[:], GC[:, :].unsqueeze(2).to_broadcast([128, 4, 128]), ALU.mult, ["P", "GC"], ["P"])
                    cp("act", Pb[:], P[:], ["P"], ["Pb"])

                def post(t0, n):
                    ti, mc = mcol(t0)
                    cp("act", YFb[:, :, 0:n], F(6, n), [Fn(6)], ["YFb"])
                    m_ps = v4(PS[2][:, 0:256], n)
                    for c in range(4):
                        mm(m_ps[:, c, :], [(cb("blk"), YFb[:, c, 0:n])], ["CSTb", "YFb"], ["B2"])
                    stt(F(5, n), m_ps, -1.0 / 64, F(6, n), ALU.mult, ALU.add, ["B2", Fn(6)], [Fn(5)])
                    act(YFb[:, :, 0:n], F(5, n), AF.Square, [Fn(5)], ["YFb"])
                    v_ps = v4(PS[3][:, 0:256], n)
                    for c in range(4):
                        mm(v_ps[:, c, :], [(cb("blk"), YFb[:, c, 0:n])], ["CSTb", "YFb"], ["B3"])
                    act(F(4, n), v_ps, AF.Ln, ["B3"], [Fn(4)], bias=GN_EPS, scale=1.0 / 64)
                    act(F(4, n), F(4, n), AF.Exp, [Fn(4)], [Fn(4)], scale=-0.5)
                    tt(F(5, n), F(5, n), F(4, n), ALU.mult, [Fn(5), Fn(4)], [Fn(5)])
                    tt(F(5, n), F(5, n), bc("rwkv_ln_w", n), ALU.mult, [Fn(5), "PP"], [Fn(5)])
                    tt(F(5, n), F(5, n), bc("rwkv_ln_b", n), ALU.add, [Fn(5), "PP"], [Fn(5)])
                    tt(F(5, n), F(5, n), F(7, n), ALU.add, [Fn(5), Fn(7)], [Fn(5)])
                    tt(MIXT[:, 4:8, mc:mc + n], F(5, n), F(11, n), ALU.mult, [Fn(5), Fn(11)], ["MIXT"])

                XSCR = [FW[:, 0, 4096 + c * 384:4096 + (c + 1) * 384] for c in range(4)]
                seq = []
                for ti, (t0, n) in enumerate(tiles):
                    seq.append(("w", ti))
                    c_ = t0
                    while c_ + CH <= min(t0 + n, TP):
                        seq.append(("c", c_))
                        c_ += CH
                    if t0 + n <= TP:
                        seq.append(("f", ti))
                done_proj = set()
                for k, ev in enumerate(seq):
                    if ev[0] == "w":
                        outproj_load(W["even_w_out"].ap()[idx][512:1024, :], 4)
                        continue
                    if ev[0] == "f":
                        outproj(W["even_w_out"].ap()[idx][512:1024, :], 4, ev[1], 4, preloaded=True)
                        continue
                    if k not in done_proj:
                        prep_proj(ev[1], CH)
                    chunk(ev[1])
                    nxt = seq[k + 1] if k + 1 < len(seq) else None
                    if nxt is not None and nxt[0] == "c":
                        prep_proj(nxt[1], CH)
                        done_proj.add(k + 1)
                    post(ev[1], CH)
                sh_f = F(4, 64)[:, :, :].rearrange("p a b -> p (a b)")
                cp("dve", sh_f[:, 0:14], PB[:, :, 0], ["PB"], [Fn(4)])
                tr(PS[4][0:14, 0:128], sh_f[:, 0:14], CSTf[:, 0:128], [Fn(4), "CSTf"], ["B4"])
                cp("act", sh_f[0:14, 128:256], PS[4][0:14, 0:128], ["B4"], [Fn(4)])
                dma("sp", sp_d.ap()[idx].rearrange("(c p) -> c p", p=128), sh_f[0:14, 128:256], [Fn(4)], [], "SHo")
                for c in range(4):
                    tr(PS[c][:, 0:128], P[:, c, :], CSTf[:, 0:128], ["P", "CSTf"], ["B%d" % c])
                    pt = F(c, 64)[:, :, :].rearrange("p a b -> p (a b)")
                    cp("act", pt[:, 0:128], PS[c][:, 0:128], ["B%d" % c], [Fn(c)])
                    for hf in range(2):
                        dma("sp", rp_d.ap()[idx][2 * c + hf], pt[hf * 64:(hf + 1) * 64, hf * 64:(hf + 1) * 64],
                            [Fn(c)], [], "RPo")
                S.barrier()

            with contextlib.ExitStack() as sst:
                SS_A = sb("SSr", [128, NS, 64], F32, sst)
                T1 = sb("T1r", [128, NS, 64], F32, sst)
                RB = sb("RBr", [128, NS, 64], BF16, sst)
                XT2 = sb("XT2", [128, 1792], F32, sst)
                SA = sb("SAr", [128, NS], F32, sst)
                n = NS
                dma("sp", XT2[0:NS, :], sts_d.ap()[idx], [], ["XT2"], "XT2")
                for c in range(14):
                    bank = PS[c % 2]
                    tr(bank[:, 0:NS], XT2[0:NS, c * 128:(c + 1) * 128], CSTf[0:NS, 0:NS], ["XT2", "CSTf"], ["B%d" % (c % 2)])
                    cp("act", PRV[:, c, 0:NS], bank[:, 0:NS], ["B%d" % (c % 2)], ["PRV"])
                prep_proj(TP, n)
                prep(TP, n, True)
                for c in range(14):
                    bank = PS[4 + (c // 4) % 4]
                    bn = "B%d" % (4 + (c // 4) % 4)
                    tr(bank[0:NS, (c % 4) * 128:(c % 4 + 1) * 128], PB[:, c, 1:1 + NS], CSTf[:, 0:128], ["PB", "CSTf"], [bn])
                    if c % 4 == 3 or c == 13:
                        c0 = (c // 4) * 4
                        w_ = (c - c0 + 1) * 128
                        cp("act", XT2[0:NS, c0 * 128:c0 * 128 + w_], bank[0:NS, 0:w_], [bn], ["XT2"])
                dma("sp", ss_d.ap()[idx], XT2[0:NS, :], ["XT2"], [], "SHo")
                xr, xv = XM[:, 0:4, 0:n], XM[:, 8:12, 0:n]
                idt = cf("idt").unsqueeze(1).to_broadcast([128, NS, 64])

                def bcast_rows(q, qn, c):
                    tt(RB[:, :, :], idt, q[:, c, :].unsqueeze(2).to_broadcast([128, NS, 64]), ALU.mult, ["CSTf", qn], ["RB"])
                    for hh in range(2):
                        mm(PS[2 + hh][:, :], [(cb("blk"), RB[:, hh * 8:(hh + 1) * 8, :].rearrange("p a b -> p (a b)"))],
                           ["CSTb", "RB"], ["B%d" % (2 + hh)])

                def psv(hh):
                    return PS[2 + hh][:, :].rearrange("p (a b) -> p a b", a=8)

                SS_B = FWf[:, 0, 2048:3072].rearrange("p (b j) -> p b j", b=NS)
                for c in range(4):
                    SS = SS_A if c % 2 == 0 else SS_B
                    ssn = "SSr%d" % (c % 2)
                    dma("sp", SS[:], str_d.ap()[idx][:, 2 * c:2 * c + 2].rearrange("b h i j -> (h i) b j"), [], [ssn], ssn)
                    bcast_rows(F(8, n), Fn(8), c)
                    for hh in range(2):
                        tt(T1[:, hh * 8:(hh + 1) * 8, :], SS[:, hh * 8:(hh + 1) * 8, :], psv(hh), ALU.mult,
                           [ssn, "B%d" % (2 + hh)], ["T1"])
                    S.op("dve", lambda e: e.tensor_reduce(out=SA[:, :], in_=T1[:, :, :], axis=AX.X, op=ALU.add, negate=True),
                         ["T1"], ["SA"])
                    bcast_rows(F(5, n), Fn(5), c)
                    for hh in range(2):
                        tt(SS[:, hh * 8:(hh + 1) * 8, :], SS[:, hh * 8:(hh + 1) * 8, :], psv(hh), ALU.mult,
                           [ssn, "B%d" % (2 + hh)], [ssn])
                    bcast_rows(F(10, n), Fn(10), c)
                    for hh in range(2):
                        tt(T1[:, hh * 8:(hh + 1) * 8, :], psv(hh), SA[:, hh * 8:(hh + 1) * 8].unsqueeze(2).to_broadcast([128, 8, 64]),
                           ALU.mult, ["SA", "B%d" % (2 + hh)], ["T1"])
                    tt(SS[:], SS[:], T1[:], ALU.add, [ssn, "T1"], [ssn])
                    bcast_rows(F(9, n), Fn(9), c)
                    for hh in range(2):
                        tt(T1[:, hh * 8:(hh + 1) * 8, :], psv(hh), xv[:, c, hh * 8:(hh + 1) * 8].unsqueeze(2).to_broadcast([128, 8, 64]),
                           ALU.mult, ["XM", "B%d" % (2 + hh)], ["T1"])
                    tt(SS[:], SS[:], T1[:], ALU.add, [ssn, "T1"], [ssn])
                    dma("pool", rs_d.ap()[idx][:, 2 * c:2 * c + 2].rearrange("b h i j -> (h i) b j"), SS[:], [ssn], [], "SSro%d" % (c % 2))
                    bcast_rows(xr, "XM", c)
                    for hh in range(2):
                        tt(T1[:, hh * 8:(hh + 1) * 8, :], SS[:, hh * 8:(hh + 1) * 8, :], psv(hh), ALU.mult,
                           [ssn, "B%d" % (2 + hh)], ["T1"])
                    S.op("dve", lambda e, c=c: e.tensor_reduce(out=F(6, n)[:, c, :], in_=T1[:, :, :], axis=AX.X, op=ALU.add),
                         ["T1"], [Fn(6)])
            with contextlib.ExitStack() as pst2:
                YFb = sb("YFb2", [128, 4, 64], BF16, pst2)
                n = NS
                ti, mc = mcol(TP)
                cp("act", YFb[:, :, 0:n], F(6, n), [Fn(6)], ["YFb"])
                m_ps = v4(PS[0][:, 0:256], n)
                for c in range(4):
                    mm(m_ps[:, c, :], [(cb("blk"), YFb[:, c, 0:n])], ["CSTb", "YFb"], ["B0"])
                stt(F(5, n), m_ps, -1.0 / 64, F(6, n), ALU.mult, ALU.add, ["B0", Fn(6)], [Fn(5)])
                act(YFb[:, :, 0:n], F(5, n), AF.Square, [Fn(5)], ["YFb"])
                v_ps = v4(PS[1][:, 0:256], n)
                for c in range(4):
                    mm(v_ps[:, c, :], [(cb("blk"), YFb[:, c, 0:n])], ["CSTb", "YFb"], ["B1"])
                act(F(4, n), v_ps, AF.Ln, ["B1"], [Fn(4)], bias=GN_EPS, scale=1.0 / 64)
                act(F(4, n), F(4, n), AF.Exp, [Fn(4)], [Fn(4)], scale=-0.5)
                tt(F(5, n), F(5, n), F(4, n), ALU.mult, [Fn(5), Fn(4)], [Fn(5)])
                tt(F(5, n), F(5, n), bc("rwkv_ln_w", n), ALU.mult, [Fn(5), "PP"], [Fn(5)])
                tt(F(5, n), F(5, n), bc("rwkv_ln_b", n), ALU.add, [Fn(5), "PP"], [Fn(5)])
                tt(F(5, n), F(5, n), F(7, n), ALU.add, [Fn(5), Fn(7)], [Fn(5)])
                tt(MIXT[:, 4:8, mc:mc + n], F(5, n), F(11, n), ALU.mult, [Fn(5), Fn(11)], ["MIXT"])
                outproj(W["even_w_out"].ap()[idx][512:1024, :], 4, NT - 1, 4, preloaded=True)

        def zero_pad_x():
            memset("dve", X[:, :, 0:PAD], 0.0, ["X0"])

        for l in range(DEPTH):
            if do_ffn and ((l % 2 == 0 and do_even) or (l % 2 == 1 and do_odd)):
                gla_load("hgrn" if l % 2 == 0 else "gla", l // 2)
            if do_ffn:
                with contextlib.ExitStack() as lst:
                    rmsnorm_to_h("norm_ffn1", l, lst)
                    ffn(l, 1, lst)
                    S.barrier()
            if (l % 2 == 0 and do_even) or (l % 2 == 1 and do_odd):
                with contextlib.ExitStack() as lst:
                    rmsnorm_to_h("norm_mix", l, lst)
                    S.barrier()
                if l % 2 == 0:
                    with contextlib.ExitStack() as lst:
                        gla_pass("hgrn", l // 2, lst)
                        S.barrier()
                    with contextlib.ExitStack() as lst:
                        rwkv_pass(l // 2, lst)
                        S.barrier()
                    zero_pad_x()
                else:
                    with contextlib.ExitStack() as lst:
                        gla_pass("gla", l // 2, lst)
                        S.barrier()
            if do_ffn:
                with contextlib.ExitStack() as lst:
                    rmsnorm_to_h("norm_ffn2", l, lst)
                    ffn(l, 2, lst)
                    S.barrier()

        with contextlib.ExitStack() as lst:
            SQ = sb("SQf", [128, 1, KD, 512], BF16, lst)
            RS = sb("RSf", [128, 2, 512], F32, lst)
            YT = sb("YT", [128, 2, D], F32, lst)
            for ti, (t0, n) in enumerate(tiles):
                sl = ti % 2
                act(SQ[:, 0, :, 0:n], X[:, :, t0:t0 + n], AF.Square, ["X%d" % ti], ["SQ0"])
                bank = PS[6 + sl]
                bn = "B%d" % (6 + sl)
                mm(bank[:, 0:n], [(cb("ones"), SQ[:, 0, c, 0:n]) for c in range(KD)], ["SQ0", "CSTb"], [bn])
                act(RS[:, sl, 0:n], bank[:, 0:n], AF.Ln, [bn], ["RS%d" % sl], bias=RMS_EPS, scale=1.0 / D)
                act(RS[:, sl, 0:n], RS[:, sl, 0:n], AF.Exp, ["RS%d" % sl], ["RS%d" % sl], scale=-0.5)
                for c in range(KD):
                    stt(X[:, c, t0:t0 + n], X[:, c, t0:t0 + n], ppc("final_norm", 0, c, 1), RS[:, sl, 0:n],
                        ALU.mult, ALU.mult, ["X%d" % ti, "RS%d" % sl, "PP"], ["X%d" % ti])
            dsts = [(yp_d.ap()[i * 128:(i + 1) * 128, :], 128, PAD + NMETA + i * 128) for i in range(S_LEN // 128)]
            dsts.append((ys_d.ap(), NS, TP))
            for i, (dst, n, t0) in enumerate(dsts):
                sl = i % 2
                for half in range(2):
                    bank = PS[(i * 2 + half) % 4]
                    bn = "B%d" % ((i * 2 + half) % 4)
                    for c4 in range(4):
                        c = half * 4 + c4
                        tr(bank[0:n, c4 * 128:(c4 + 1) * 128], X[:, c, t0:t0 + n], CSTf[:, 0:128],
                           xres(t0, n) + ["CSTf"], [bn])
                    cp("act" if half == 0 else "dve", YT[0:n, sl, half * 512:(half + 1) * 512], bank[0:n, :],
                       [bn], ["YT%d_%d" % (sl, half)])
                dma("sp", dst, YT[0:n, sl, :], ["YT%d_0" % sl, "YT%d_1" % sl], [], "YTo%d" % sl)
        S.emit(st)
    return nc


_NC_CACHE = {}


def kernel(**inputs):
    inputs = {k: np.asarray(v) for k, v in inputs.items()}
    depth = inputs["norm_ffn1"].shape[0]
    B, S_LEN, _ = inputs["x_prompt"].shape
    n_cores = 8
    NS = inputs["x_sample"].shape[0] // n_cores
    key = (S_LEN, NS, depth)
    if key not in _NC_CACHE:
        _NC_CACHE[key] = build(S_LEN, NS, depth)
    nc = _NC_CACHE[key]
    pp = pack_pp(inputs, depth)
    cst = make_cst()
    shared = {"pp": pp, "cst": cst, "meta": np.ascontiguousarray(inputs["meta_tokens"], np.float32)}
    for nm in ("ffn1_gate", "ffn1_up", "ffn1_down", "ffn2_gate", "ffn2_up", "ffn2_down", "even_w_in", "even_w_out",
               "odd_w_in", "odd_w_out", "rwkv_w2", "rwkv_a2", "rwkv_g2", "gla_gate_up"):
        shared[nm] = np.ascontiguousarray(inputs[nm], np.float32)
    in_maps = []
    for c in range(n_cores):
        m = dict(shared)
        m["xp"] = np.ascontiguousarray(inputs["x_prompt"][c])
        m["xs"] = np.ascontiguousarray(inputs["x_sample"][c * NS:(c + 1) * NS, 0, :])
        m["st_hgrn"] = np.ascontiguousarray(inputs["state_hgrn"][:, c * NS:(c + 1) * NS])
        m["st_rwkv"] = np.ascontiguousarray(inputs["state_rwkv"][:, c * NS:(c + 1) * NS])
        m["st_shift"] = np.ascontiguousarray(inputs["state_rwkv_shift"][:, c * NS:(c + 1) * NS])
        m["st_gla"] = np.ascontiguousarray(inputs["state_gla"][:, c * NS:(c + 1) * NS])
        in_maps.append(m)
    res = run_bass_kernel_spmd(nc, in_maps, core_ids=list(range(n_cores)))
    R = res.results
    yp = np.stack([R[c]["y_prompt"] for c in range(n_cores)], 0)
    ys = np.concatenate([R[c]["y_sample"] for c in range(n_cores)], 0)[:, None, :]
    hp = np.stack([R[c]["hgrn_prompt"] for c in range(n_cores)], 1)
    rp = np.stack([R[c]["rwkv_prompt"] for c in range(n_cores)], 1)
    sp = np.stack([R[c]["shift_prompt"] for c in range(n_cores)], 1)
    gp = np.stack([R[c]["gla_prompt"] for c in range(n_cores)], 1)
    hs = np.concatenate([R[c]["hgrn_sample"] for c in range(n_cores)], 1)
    rs = np.concatenate([R[c]["rwkv_sample"] for c in range(n_cores)], 1)
    ss = np.concatenate([R[c]["shift_sample"] for c in range(n_cores)], 1)
    gs = np.concatenate([R[c]["gla_sample"] for c in range(n_cores)], 1)
    return tuple(np.ascontiguousarray(a, dtype=np.float32) for a in (yp, ys, hp, rp, sp, gp, hs, rs, ss, gs))
```

```python
import contextlib
import numpy as np
import concourse.bass as bass
import concourse.mybir as mybir
from concourse.bass_utils import run_bass_kernel_spmd

F32 = mybir.dt.float32
BF16 = mybir.dt.bfloat16
ALU = mybir.AluOpType
AF = mybir.ActivationFunctionType
AX = mybir.AxisListType

ENGINES = ("pe", "act", "dve", "pool", "sp")
EPOCH = 30000

D = 1024
KD = 8
DFF = 2816
PAD = 48
NMETA = 16
CH = 64
RMS_EPS = 1e-6
GN_EPS = 64e-5
DEBUG_STAGE = 9


class Sched:
    def __init__(self, nc):
        self.nc = nc
        self.ops = {e: [] for e in ENGINES}
        self.seq = {e: 0 for e in ENGINES}
        self.res = {}
        self.waited = {e: {} for e in ENGINES}
        self.dma_cnt = {}
        self.sem_handles = {}

    def _tok_for_seq(self, eng, seq):
        return (("e", eng, (seq - 1) // EPOCH), (seq - 1) % EPOCH + 1)

    def op(self, eng, fn, reads=(), writes=(), dma=None):
        deps = {}

        def add(tok):
            if tok is None:
                return
            k, v = tok
            if deps.get(k, 0) < v:
                deps[k] = v

        for r in reads:
            st = self.res.get(r)
            if st is not None:
                add(st["w"])
        for w in writes:
            st = self.res.get(w)
            if st is not None:
                add(st["w"])
                for t in st["r"].items():
                    add(t)
        waits = []
        for k, v in deps.items():
            if k[0] == "e" and k[1] == eng and eng == "pe":
                continue
            if self.waited[eng].get(k, 0) >= v:
                continue
            self.waited[eng][k] = v
            waits.append((k, v))
        if dma is not None:
            key = ("d", dma)
            self.dma_cnt[dma] = self.dma_cnt.get(dma, 0) + 16
            tok = (key, self.dma_cnt[dma])
            inc = (key, 16)
        else:
            self.seq[eng] += 1
            tok = self._tok_for_seq(eng, self.seq[eng])
            inc = (tok[0], 1)
        self.ops[eng].append((waits, fn, inc))
        for w in writes:
            self.res[w] = {"w": tok, "r": {}}
        for r in reads:
            if r in writes:
                continue
            st = self.res.setdefault(r, {"w": None, "r": {}})
            if st["r"].get(tok[0], 0) < tok[1]:
                st["r"][tok[0]] = tok[1]
        return tok

    def barrier(self):
        toks = []
        for name, cnt in self.dma_cnt.items():
            toks.append((("d", name), cnt))
        for e in ENGINES:
            if self.seq[e] > 0:
                toks.append(self._tok_for_seq(e, self.seq[e]))
        for e in ENGINES:
            waits = []
            for k, v in toks:
                if k[0] == "e" and k[1] == e:
                    continue
                if self.waited[e].get(k, 0) >= v:
                    continue
                self.waited[e][k] = v
                waits.append((k, v))
            if waits:
                self.ops[e].append((waits, None, None))

    def emit(self, stack):
        nc = self.nc
        self.barrier()
        keys = set()
        for e in ENGINES:
            for waits, fn, inc in self.ops[e]:
                if inc is not None:
                    keys.add(inc[0])
                for k, v in waits:
                    keys.add(k)
        for k in sorted(keys, key=str):
            nm = "s_" + "_".join(str(x) for x in k)
            self.sem_handles[k] = stack.enter_context(nc.semaphore(nm))
        block = stack.enter_context(nc.Block())
        Hd = self.sem_handles

        def replay(ename):
            def body(eng):
                for waits, fn, inc in self.ops[ename]:
                    for k, v in waits:
                        eng.wait_ge(Hd[k], v)
                    if fn is None:
                        continue
                    ins = fn(eng)
                    ins.then_inc(Hd[inc[0]], inc[1])
            return body

        block.tensor(replay("pe"))
        block.scalar(replay("act"))
        block.vector(replay("dve"))
        block.gpsimd(replay("pool"))
        block.sync(replay("sp"))


PP_SPEC = [
    ("norm_ffn1", "L", 8), ("norm_mix", "L", 8), ("norm_ffn2", "L", 8), ("final_norm", None, 8),
    ("hgrn_lb", "E", 4), ("hgrn_norm", "E", 4), ("rwkv_mu", "E", 14),
    ("rwkv_w0", "E", 4), ("rwkv_a0", "E", 4), ("rwkv_kk", "E", 4), ("rwkv_ka", "E", 4),
    ("rwkv_rk", "E", 4), ("rwkv_ln_w", "E", 4), ("rwkv_ln_b", "E", 4),
    ("gla_gate_b", "O", 4), ("gla_norm", "O", 2),
]


def pp_layout(depth):
    n_even = (depth + 1) // 2
    n_odd = depth // 2
    off = {}
    o = 0
    for name, kind, nch in PP_SPEC:
        cnt = {"L": depth, "E": n_even, "O": n_odd, None: 1}[kind]
        off[name] = (o, nch)
        o += cnt * nch
    return off, o


def pack_pp(inputs, depth):
    off, n = pp_layout(depth)
    pp = np.zeros((128, n), np.float32)
    for name, kind, nch in PP_SPEC:
        a = np.asarray(inputs[name], np.float32)
        a = a.reshape(-1, nch, 128)
        o = off[name][0]
        pp[:, o:o + a.shape[0] * nch] = a.transpose(2, 0, 1).reshape(128, -1)
    return pp


CST_SPEC = [("ident", 128), ("ones", 128), ("causT", 64), ("mt2", 256), ("mstr", 128), ("idt", 64), ("blk", 128),
            ("sel", 16 * 128)]
NCF = 128 + 128 + 64 + 256 + 128 + 64


def cst_layout():
    off = {}
    o = 0
    for name, n in CST_SPEC:
        off[name] = o
        o += n
    return off, o


def make_cst():
    off, n = cst_layout()
    c = np.zeros((128, n), np.float32)
    c[:, off["ident"]:off["ident"] + 128] = np.eye(128)
    c[:, off["ones"]:off["ones"] + 128] = 1.0
    s = np.arange(64)
    causT = (s[:, None] <= s[None, :]).astype(np.float32)
    c[:64, off["causT"]:off["causT"] + 64] = causT
    strT = (s[:, None] < s[None, :]).astype(np.float32)
    mstrT = np.zeros((128, 128), np.float32)
    mincT = np.zeros((128, 128), np.float32)
    blk = np.zeros((128, 128), np.float32)
    for h in range(2):
        mstrT[h * 64:(h + 1) * 64, h * 64:(h + 1) * 64] = strT
        mincT[h * 64:(h + 1) * 64, h * 64:(h + 1) * 64] = causT
        blk[h * 64:(h + 1) * 64, h * 64:(h + 1) * 64] = 1.0
    c[:, off["mt2"]:off["mt2"] + 128] = mstrT
    c[:, off["mt2"] + 128:off["mt2"] + 256] = mincT
    c[:, off["mstr"]:off["mstr"] + 128] = mstrT.T
    c[:, off["blk"]:off["blk"] + 128] = blk
    c[:64, off["idt"]:off["idt"] + 64] = np.eye(64)
    c[64:, off["idt"]:off["idt"] + 64] = np.eye(64)
    sel = np.zeros((128, 16, 128), np.float32)
    for b in range(16):
        sel[b, b, :] = 1.0
    c[:, off["sel"]:off["sel"] + 16 * 128] = sel.reshape(128, -1)
    return c


WEIGHT_SHAPES = {
    "ffn1_gate": lambda L: [L, D, DFF], "ffn1_up": lambda L: [L, D, DFF], "ffn1_down": lambda L: [L, DFF, D],
    "ffn2_gate": lambda L: [L, D, DFF], "ffn2_up": lambda L: [L, D, DFF], "ffn2_down": lambda L: [L, DFF, D],
}


def build(S_LEN=2048, NS=16, DEPTH=4, do_even=True, do_odd=True, do_ffn=True):
    NE = (DEPTH + 1) // 2
    NO = DEPTH // 2
    TP = PAD + NMETA + S_LEN
    assert TP % CH == 0 and S_LEN % 128 == 0
    NCHK = TP // CH
    TT = TP + NS
    tiles = []
    t = 0
    while t < TT:
        rem = TT - t
        if 512 < rem < 1024:
            n = ((rem + 1) // 2 + CH - 1) // CH * CH
        else:
            n = min(512, rem)
        tiles.append((t, n))
        t += n
    NT = len(tiles)
    ppo, NPP = pp_layout(DEPTH)
    cso, NCST = cst_layout()

    nc = bass.Bass("TRN2", target_bir_lowering=False)

    def din(name, shape):
        return nc.dram_tensor(name, list(shape), F32, kind="ExternalInput")

    def dout(name, shape):
        return nc.dram_tensor(name, list(shape), F32, kind="ExternalOutput")

    xp_d = din("xp", [S_LEN, D])
    xs_d = din("xs", [NS, D])
    meta_d = din("meta", [NMETA, D])
    pp_d = din("pp", [128, NPP])
    cst_d = din("cst", [128, NCST])
    W = {}
    for nm in ("ffn1_gate", "ffn1_up", "ffn2_gate", "ffn2_up"):
        W[nm] = din(nm, [DEPTH, D, DFF])
    for nm in ("ffn1_down", "ffn2_down"):
        W[nm] = din(nm, [DEPTH, DFF, D])
    W["even_w_in"] = din("even_w_in", [NE, D, 3840])
    W["even_w_out"] = din("even_w_out", [NE, 1024, D])
    W["odd_w_in"] = din("odd_w_in", [max(NO, 1), D, 3088])
    W["odd_w_out"] = din("odd_w_out", [max(NO, 1), 1024, D])
    W["rwkv_w2"] = din("rwkv_w2", [NE, 64, 512])
    W["rwkv_a2"] = din("rwkv_a2", [NE, 64, 512])
    W["rwkv_g2"] = din("rwkv_g2", [NE, 128, 512])
    W["gla_gate_up"] = din("gla_gate_up", [max(NO, 1), 16, 512])
    sth_d = din("st_hgrn", [NE, NS, 4, 128, 128])
    str_d = din("st_rwkv", [NE, NS, 8, 64, 64])
    sts_d = din("st_shift", [NE, NS, 1792])
    stg_d = din("st_gla", [max(NO, 1), NS, 4, 128, 256])

    yp_d = dout("y_prompt", [S_LEN, D])
    ys_d = dout("y_sample", [NS, D])
    hp_d = dout("hgrn_prompt", [NE, 4, 128, 128])
    rp_d = dout("rwkv_prompt", [NE, 8, 64, 64])
    sp_d = dout("shift_prompt", [NE, 1792])
    gp_d = dout("gla_prompt", [max(NO, 1), 4, 128, 256])
    hs_d = dout("hgrn_sample", [NE, NS, 4, 128, 128])
    rs_d = dout("rwkv_sample", [NE, NS, 8, 64, 64])
    ss_d = dout("shift_sample", [NE, NS, 1792])
    gs_d = dout("gla_sample", [max(NO, 1), NS, 4, 128, 256])

    with contextlib.ExitStack() as st:
        _uid = [0]

        def sb(name, shape, dt, stack=st):
            _uid[0] += 1
            return stack.enter_context(nc.sbuf_tensor("%s_%d" % (name, _uid[0]), list(shape), dt))

        X = sb("X", [128, KD, TT], F32)
        Hb = sb("Hb", [128, KD, TT + 112], BF16)
        MW = sb("MW", [128, 17024], BF16)
        FW = sb("FW", [128, 2, 6144], BF16)
        PPt = sb("PPt", [128, NPP], F32)
        CSTf = sb("CSTf", [128, NCF], F32)
        CSTb = sb("CSTb", [128, NCST], BF16)
        ONESf = sb("ONESf", [128, 64], F32)
        MIXT = sb("MIXT", [128, 8, 512], BF16)
        PS = [st.enter_context(nc.psum_tensor("PS%d" % i, [128, 512], F32)) for i in range(8)]
        PSb = [p.bitcast(BF16) for p in PS]

        S = Sched(nc)

        def cb(name, n=None, rows=128):
            o = cso[name]
            if n is None:
                n = dict(CST_SPEC)[name]
            return CSTb[0:rows, o:o + n]

        def cf(name, n=None, rows=128):
            o = cso[name]
            if n is None:
                n = dict(CST_SPEC)[name]
            return CSTf[0:rows, o:o + n]

        def ppc(name, idx, c0=0, n=None):
            o, nch = ppo[name]
            if n is None:
                n = nch
            return PPt[:, o + idx * nch + c0:o + idx * nch + c0 + n]

        def act(out, in_, func, reads, writes, bias=0.0, scale=1.0):
            S.op("act", lambda e: e.activation(out=out, in_=in_, func=func, bias=bias, scale=scale), reads, writes)

        def tt(out, a, b, op, reads, writes, eng="dve"):
            S.op(eng, lambda e: e.tensor_tensor(out=out, in0=a, in1=b, op=op), reads, writes)

        def ts(out, a, s1, s2, op0, op1, reads, writes, eng="dve"):
            if s2 is None:
                S.op(eng, lambda e: e.tensor_scalar(out=out, in0=a, scalar1=s1, scalar2=None, op0=op0), reads, writes)
            else:
                S.op(eng, lambda e: e.tensor_scalar(out=out, in0=a, scalar1=s1, scalar2=s2, op0=op0, op1=op1),
                     reads, writes)

        def stt(out, a, s, b, op0, op1, reads, writes):
            S.op("dve", lambda e: e.scalar_tensor_tensor(out=out, in0=a, scalar=s, in1=b, op0=op0, op1=op1),
                 reads, writes)

        def cp(eng, out, in_, reads, writes):
            if eng == "act":
                S.op("act", lambda e: e.activation(out=out, in_=in_, func=AF.Copy), reads, writes)
            else:
                S.op(eng, lambda e: e.tensor_copy(out=out, in_=in_), reads, writes)

        def mm(out, pairs, reads, writes):
            def fn(e):
                n = len(pairs)
                ins = None
                for i, (l, r) in enumerate(pairs):
                    ins = e.matmul(out, l, r, start=(i == 0), stop=(i == n - 1))
                return ins
            S.op("pe", fn, reads, writes)

        def tr(out, in_, ident, reads, writes):
            S.op("pe", lambda e: e.transpose(out, in_, ident), reads, writes)

        def dma(eng, out, in_, reads, writes, sem):
            S.op(eng, lambda e: e.dma_start(out=out, in_=in_), reads, writes, dma=sem)

        def memset(eng, ap, val, writes):
            S.op(eng, lambda e: e.memset(ap, val), (), writes)

        def sigmoid_chain(out, in_, tmp, reads, writes, tmpname, scale_in=1.0):
            act(tmp, in_, AF.Exp, reads, [tmpname], scale=-scale_in)
            act(tmp, tmp, AF.Ln, [tmpname], [tmpname], bias=1.0)
            act(out, tmp, AF.Exp, [tmpname], writes, scale=-1.0)

        dma("sp", PPt[:], pp_d.ap(), [], ["PP"], "PP")
        dma("sp", CSTf[:], cst_d.ap()[:, 0:NCF], [], ["CSTf"], "CSTf")
        for c0 in range(0, NCST, 1024):
            c1 = min(NCST, c0 + 1024)
            dma("pool", CSTb[:, c0:c1], cst_d.ap()[:, c0:c1], [], ["CSTb"], "CSTb")
        memset("dve", ONESf[:], 1.0, ["ONESf"])
        memset("dve", X[:, :, 0:PAD], 0.0, ["X0"])
        memset("dve", Hb[:, :, TT:TT + 112], 0.0, ["Hpad"])

        def xres(t0, n):
            return ["X%d" % i for i, (a, m) in enumerate(tiles) if a < t0 + n and t0 < a + m]

        def hres(t0, n):
            return ["H%d" % i for i, (a, m) in enumerate(tiles) if a < t0 + n and t0 < a + m]

        with contextlib.ExitStack() as lst:
            XT = sb("XT", [128, 2, D], F32, lst)
            srcs = [(meta_d.ap(), NMETA, PAD)]
            for i in range(S_LEN // 128):
                srcs.append((xp_d.ap()[i * 128:(i + 1) * 128, :], 128, PAD + NMETA + i * 128))
            srcs.append((xs_d.ap(), NS, TP))
            for i, (src, n, t0) in enumerate(srcs):
                sl = i % 2
                dma("sp", XT[0:n, sl, :], src, [], ["XT%d" % sl], "XT%d" % sl)
                for half in range(2):
                    bank = PS[(i * 2 + half) % 4]
                    bn = "B%d" % ((i * 2 + half) % 4)
                    for c4 in range(4):
                        c = half * 4 + c4
                        tr(bank[:, c4 * 128:c4 * 128 + n], XT[0:n, sl, c * 128:(c + 1) * 128], CSTf[0:n, 0:n],
                           ["XT%d" % sl, "CSTf"], [bn])
                    src_ap = bank[:, :].rearrange("p (a b) -> p a b", a=4)[:, :, 0:n]
                    cp("act" if half == 0 else "dve", X[:, half * 4:half * 4 + 4, t0:t0 + n], src_ap,
                       [bn], xres(t0, n))
        S.barrier()

        def rmsnorm_to_h(pname, pidx, lst):
            SQ = sb("SQ", [128, 1, KD, 512], BF16, lst)
            RS = sb("RS", [128, 2, 512], F32, lst)
            for ti, (t0, n) in enumerate(tiles):
                sl = ti % 2
                act(SQ[:, 0, :, 0:n], X[:, :, t0:t0 + n], AF.Square, ["X%d" % ti], ["SQ0"])
                bank = PS[6 + sl]
                bn = "B%d" % (6 + sl)
                mm(bank[:, 0:n], [(cb("ones"), SQ[:, 0, c, 0:n]) for c in range(KD)], ["SQ0", "CSTb"], [bn])
                act(RS[:, sl, 0:n], bank[:, 0:n], AF.Ln, [bn], ["RS%d" % sl], bias=RMS_EPS, scale=1.0 / D)
                act(RS[:, sl, 0:n], RS[:, sl, 0:n], AF.Exp, ["RS%d" % sl], ["RS%d" % sl], scale=-0.5)
                for c in range(KD):
                    stt(Hb[:, c, t0:t0 + n], X[:, c, t0:t0 + n], ppc(pname, pidx, c, 1), RS[:, sl, 0:n],
                        ALU.mult, ALU.mult, ["X%d" % ti, "RS%d" % sl, "PP"], ["H%d" % ti])

        def ffn(l, which, lst):
            gate_d = W["ffn%d_gate" % which].ap()[l].rearrange("(k p) f -> p k f", p=128)
            up_d = W["ffn%d_up" % which].ap()[l].rearrange("(k p) f -> p k f", p=128)
            down_d = W["ffn%d_down" % which].ap()[l]
            SIL = sb("SIL", [128, 2, 512], F32, lst)
            G = sb("G", [128, 2, 2, 512], BF16, lst)
            NG = DFF // 256
            views = {}

            def load(g):
                sl = g % 2
                f0 = g * 256
                wn = "FW%d" % sl
                gv = FW[:, sl, 0:2048].rearrange("p (k f) -> p k f", k=8)
                uv = FW[:, sl, 2048:4096].rearrange("p (k f) -> p k f", k=8)
                dv = FW[:, sl, 4096:6144].rearrange("p (j m) -> p j m", j=2)
                dma("pool", gv, gate_d[:, :, f0:f0 + 256], [], [wn], wn)
                dma("pool", uv, up_d[:, :, f0:f0 + 256], [], [wn], wn)
                dma("pool", dv, down_d[f0:f0 + 256, :].rearrange("(j p) m -> p j m", p=128), [], [wn], wn)
                views[g] = (gv, uv, dv, wn)

            def gate_up(it):
                g, ti, gs = it
                gv, uv, dv, wn = views[g]
                t0, n = tiles[ti]
                for j in range(2):
                    bg, bu = PS[j], PS[2 + j]
                    mm(bg[:, 0:n], [(gv[:, k, j * 128:(j + 1) * 128], Hb[:, k, t0:t0 + n]) for k in range(KD)],
                       [wn, "H%d" % ti], ["B%d" % j])
                    mm(bu[:, 0:n], [(uv[:, k, j * 128:(j + 1) * 128], Hb[:, k, t0:t0 + n]) for k in range(KD)],
                       [wn, "H%d" % ti], ["B%d" % (2 + j)])
                    act(SIL[:, j, 0:n], bg[:, 0:n], AF.Silu, ["B%d" % j], ["SIL%d" % j])
                    tt(G[:, gs, j, 0:n], SIL[:, j, 0:n], bu[:, 0:n], ALU.mult, ["SIL%d" % j, "B%d" % (2 + j)],
                       ["G%d_%d" % (gs, j)])

            def down(it):
                g, ti, gs = it
                gv, uv, dv, wn = views[g]
                t0, n = tiles[ti]
                for dm in range(KD):
                    bd = PS[4 + dm % 4]
                    bn = "B%d" % (4 + dm % 4)
                    mm(bd[:, 0:n], [(dv[:, j, dm * 128:(dm + 1) * 128], G[:, gs, j, 0:n]) for j in range(2)],
                       [wn, "G%d_0" % gs, "G%d_1" % gs], [bn])
                    stt(X[:, dm, t0:t0 + n], bd[:, 0:n], 0.5, X[:, dm, t0:t0 + n], ALU.mult, ALU.add,
                        [bn, "X%d" % ti], ["X%d" % ti])

            items = []
            for g in range(NG):
                for ti in range(NT):
                    items.append((g, ti, len(items) % 2))
            loaded = set()
            for i in range(len(items) + 1):
                if i < len(items):
                    g = items[i][0]
                    if g not in loaded:
                        load(g)
                        loaded.add(g)
                    gate_up(items[i])
                if i >= 1:
                    down(items[i - 1])

        def mcol(t0):
            for ti, (a, m) in enumerate(tiles):
                if a <= t0 < a + m:
                    return ti, t0 - a
            raise ValueError

        def outproj_load(wrows, nk):
            for part in range(nk // 4):
                wn = "FW%d" % part
                wv = FW[:, part, 0:4096].rearrange("p (j m) -> p j m", j=4)
                dma("pool", wv, wrows[part * 512:(part + 1) * 512, :].rearrange("(j p) m -> p j m", p=128), [], [wn], wn)

        def outproj(wrows, nk, ti, mix_c0, preloaded=False):
            t0, n = tiles[ti]
            if not preloaded:
                outproj_load(wrows, nk)
            for dm in range(KD):
                bank = PS[dm % 2]
                bn = "B%d" % (dm % 2)
                pairs = []
                for j in range(nk):
                    wv = FW[:, j // 4, 0:4096].rearrange("p (j m) -> p j m", j=4)
                    pairs.append((wv[:, j % 4, dm * 128:(dm + 1) * 128], MIXT[:, mix_c0 + j, 0:n]))
                mm(bank[:, 0:n], pairs, ["FW0", "FW1", "MIXT"], [bn])
                stt(X[:, dm, t0:t0 + n], bank[:, 0:n], 1.0, X[:, dm, t0:t0 + n], ALU.mult, ALU.add,
                    [bn, "X%d" % ti], ["X%d" % ti])

        preloaded_mw = set()

        def gla_load(kind, idx):
            is_h = kind == "hgrn"
            ncol = 2048 if is_h else 2064
            win = (W["even_w_in"] if is_h else W["odd_w_in"]).ap()[idx].rearrange("(k p) f -> p k f", p=128)
            WA = MW[:, 0:8 * ncol].rearrange("p (k f) -> p k f", k=8)
            for c0 in range(0, ncol, 512):
                c1 = min(ncol, c0 + 512)
                dma("pool", WA[:, :, c0:c1], win[:, :, c0:c1], [], ["MW"], "MW")
            if not is_h:
                dma("pool", MW[0:16, 16512:17024], W["gla_gate_up"].ap()[idx], [], ["MW"], "MW")
            preloaded_mw.add((kind, idx))

        def gla_pass(kind, idx, lst):
            is_h = kind == "hgrn"
            dv = 128 if is_h else 256
            NV = 4 * dv
            nvc = dv // 128
            esc = 1.0 if is_h else -1.0 / 16.0
            ncol = 2048 if is_h else 2064
            WA = MW[:, 0:8 * ncol].rearrange("p (k f) -> p k f", k=8)
            GU = MW[0:16, 16512:17024]
            if (kind, idx) not in preloaded_mw:
                gla_load(kind, idx)
            WK = sb("WK", [128, 10, 256], F32, lst)
            QTb = sb("QTb", [128, 4, 64], BF16, lst)
            KTb = sb("KTb", [128, 4, 128], BF16, lst)
            KDb = sb("KDb", [128, 4, 128], BF16, lst)
            Vb = sb("Vb", [128, NV], BF16, lst)
            KDT = sb("KDT", [128, 512], BF16, lst)
            ATb = sb("ATb", [128, 4, 64], BF16, lst)
            TV = sb("TV", [128, 512], F32, lst)
            SQb = sb("SQb", [128, 4 * nvc, 64], BF16, lst)
            RSn = sb("RSn", [128, 4, 64], F32, lst)
            ST = sb("ST", [128, 4, dv], F32, lst)
            STb = sb("STb", [128, 4, dv], BF16, lst)
            GDb = sb("GDb", [128, 64], BF16, lst)
            LB = sb("LB", [128, 8], F32, lst)
            SS = sb("SS", [128, 1, 4, dv], F32, lst)
            QS = sb("QS", [128, 4, 16], F32, lst)
            KS = sb("KS", [128, 4, 16], F32, lst)
            memset("dve", ST[:], 0.0, ["ST"])
            memset("dve", STb[:], 0.0, ["STb"])
            memset("dve", KTb[:], 0.0, ["KTb"])
            memset("dve", KDb[:], 0.0, ["KDb"])
            if is_h:
                if idx == 0:
                    memset("dve", LB[:, 0:4], 0.0, ["LB"])
                else:
                    assert NE == 2
                    tt(LB[:, 4:8], ppc("hgrn_lb", 0), ppc("hgrn_lb", 1), ALU.subtract, ["PP"], ["LBt"])
                    sigmoid_chain(LB[:, 0:4], LB[:, 4:8], LB[:, 4:8], ["LBt"], ["LB"], "LBt", scale_in=-1.0)
                ts(LB[:, 4:8], LB[:, 0:4], -1.0, 1.0, ALU.mult, ALU.add, ["LB"], ["OML"])

            def v3(ap2, n, nh=4):
                return ap2.rearrange("p (h t) -> p h t", h=nh)[:, :, 0:n]

            def wk(i, n, nh=4):
                return WK[:, i, :].rearrange("p (h t) -> p h t", h=nh)[:, :, 0:n]

            def prep_proj(t0, n):
                tok = slice(t0, t0 + n)
                hr = hres(t0, n)
                q_ps = v3(PS[0][:, 0:256], n)
                f_ps = v3(PS[0][:, 256:512], n)
                g_ps = v3(PS[1][:, 0:256], n)
                for h in range(4):
                    mm(q_ps[:, h, :], [(WA[:, k, h * 128:(h + 1) * 128], Hb[:, k, tok]) for k in range(KD)],
                       ["MW"] + hr, ["B0"])
                for h in range(4):
                    mm(f_ps[:, h, :], [(WA[:, k, 512 + h * 128:512 + (h + 1) * 128], Hb[:, k, tok]) for k in range(KD)],
                       ["MW"] + hr, ["B0"])
                if is_h:
                    for h in range(4):
                        mm(g_ps[:, h, :], [(WA[:, k, 1536 + h * 128:1536 + (h + 1) * 128], Hb[:, k, tok])
                                           for k in range(KD)], ["MW"] + hr, ["B1"])
                else:
                    gd_ps = PS[1][0:16, 256:256 + n]
                    mm(gd_ps, [(WA[:, k, 2048:2064], Hb[:, k, tok]) for k in range(KD)], ["MW"] + hr, ["B1"])
                    cp("act", GDb[0:16, 0:n], gd_ps, ["B1"], ["GDb"])
                    for h in range(4):
                        mm(g_ps[:, h, :], [(GU[:, h * 128:(h + 1) * 128], GDb[0:16, 0:n])], ["MW", "GDb"], ["B1"])
                for j in range(NV // 512):
                    mm(PS[2 + j][:, :], [(Hb[:, k, t0:t0 + 128], WA[:, k, 1024 + j * 512:1024 + (j + 1) * 512])
                                         for k in range(KD)], ["MW", "Hpad"] + hres(t0, 128), ["B%d" % (2 + j)])

            def prep(t0, n, sample):
                q_ps = v3(PS[0][:, 0:256], n)
                f_ps = v3(PS[0][:, 256:512], n)
                g_ps = v3(PS[1][:, 0:256], n)
                vr = n if sample else 128
                if is_h:
                    sigmoid_chain(wk(1, n), f_ps, wk(0, n), ["B0"], ["T1"], "T0")
                    act(TV[0:vr, 0:NV], PS[2][0:vr, :], AF.Exp, ["B2"], ["TV"], scale=-1.0)
                    act(TV[0:vr, 0:NV], TV[0:vr, 0:NV], AF.Ln, ["TV"], ["TV"], bias=1.0)
                    act(TV[0:vr, 0:NV], TV[0:vr, 0:NV], AF.Exp, ["TV"], ["TV"], scale=-1.0)
                    tt(wk(2, n), wk(1, n), LB[:, 4:8].unsqueeze(2).to_broadcast([128, 4, n]), ALU.mult, ["T1", "OML"], ["T2"])
                    tt(wk(2, n), wk(2, n), LB[:, 0:4].unsqueeze(2).to_broadcast([128, 4, n]), ALU.add, ["T2", "LB"], ["T2"])
                    ts(wk(3, n), wk(2, n), -1.0, 1.0, ALU.mult, ALU.add, ["T2"], ["T3"])
                    act(wk(4, n), wk(2, n), AF.Ln, ["T2"], ["T4"])
                    tt(Vb[0:vr, 0:NV], PS[2][0:vr, :], TV[0:vr, 0:NV], ALU.mult, ["B2", "TV"], ["Vb"])
                    sigmoid_chain(wk(1, n), g_ps, wk(0, n), ["B1"], ["T1"], "T0")
                    kf, kfr = wk(3, n), ["T3"]
                else:
                    gb = ppc("gla_gate_b", idx).unsqueeze(2).to_broadcast([128, 4, n])
                    tt(wk(2, n), g_ps, gb, ALU.add, ["B1", "PP"], ["T2"])
                    act(wk(0, n), wk(2, n), AF.Exp, ["T2"], ["T0"], scale=-1.0)
                    act(wk(4, n), wk(0, n), AF.Ln, ["T0"], ["T4"], bias=1.0)
                    for j in range(2):
                        cp("act", Vb[0:vr, j * 512:(j + 1) * 512], PS[2 + j][0:vr, :], ["B%d" % (2 + j)], ["Vb"])
                    kf, kfr = f_ps, ["B0"]
                if not sample:
                    for h in range(4):
                        S.op("dve", lambda e, h=h: e.tensor_tensor_scan(out=wk(5, n)[:, h, :], data0=ONESf[:, 0:n],
                                                                      data1=wk(4, n)[:, h, :], initial=0.0,
                                                                      op0=ALU.mult, op1=ALU.add),
                             ["T4", "ONESf"], ["T5"])
                    gc, gcr = wk(5, n), ["T5"]
                else:
                    gc, gcr = wk(4, n), ["T4"]
                act(wk(6, n), gc, AF.Exp, gcr, ["T6"], scale=esc)
                if sample:
                    ts(QS[:, :, 0:n], q_ps, 128.0 ** -0.5, None, ALU.mult, None, ["B0"], ["QS"])
                    cp("dve", KS[:, :, 0:n], kf, kfr, ["KS"])
                    return
                act(wk(7, n), gc, AF.Exp, gcr, ["T7"], scale=-esc)
                stt(QTb[:, :, 0:n], q_ps, 128.0 ** -0.5, wk(6, n), ALU.mult, ALU.mult, ["B0", "T6"], ["QTb"])
                tt(KTb[:, :, 0:n], wk(7, n), kf, ALU.mult, ["T7"] + kfr, ["KTb"])
                for h in range(4):
                    stt(KDb[:, h, 0:n], wk(7, n)[:, h, :], wk(6, n)[:, h, n - 1:n], kf[:, h, :], ALU.mult, ALU.mult,
                        ["T7", "T6"] + kfr, ["KDb"])

            def recur(n):
                sc_ps = v3(PS[4][:, 0:256], n)
                for h in range(4):
                    mm(sc_ps[:, h, :], [(KTb[:, h, :], QTb[:, h, 0:n])], ["KTb", "QTb"], ["B4a"])
                for h in range(4):
                    tr(PSb[4][:, 512 + h * 128:512 + (h + 1) * 128], KDb[:, h, :], cb("ident"), ["KDb", "CSTb"], ["B4b"])
                cp("act", wk(8, n), sc_ps, ["B4a"], ["T8"])
                cp("act", KDT[:, :], PSb[4][:, 512:1024], ["B4b"], ["KDT"])
                tt(ATb[:, :, 0:n], wk(8, n), cf("causT")[:, 0:n].unsqueeze(1).to_broadcast([128, 4, n]), ALU.mult,
                   ["T8", "CSTf"], ["ATb"])
                if DEBUG_STAGE < 1.7:
                    return
                o_ps = v3(PS[5][:, 0:4 * nvc * 64], n, 4 * nvc)
                for h in range(4):
                    for vc in range(nvc):
                        c0 = h * dv + vc * 128
                        mm(o_ps[:, h * nvc + vc, :], [(Vb[:, c0:c0 + 128], ATb[:, h, 0:n]),
                                                      (STb[:, h, vc * 128:(vc + 1) * 128], QTb[:, h, 0:n])],
                           ["Vb", "ATb", "STb", "QTb"], ["B5"])
                if DEBUG_STAGE < 1.9:
                    return
                for h in range(4):
                    bank = PS[6 + (h * dv) // 512]
                    c0 = (h * dv) % 512
                    mm(bank[:, c0:c0 + dv], [(KDT[:, h * 128:(h + 1) * 128], Vb[:, h * dv:(h + 1) * dv])],
                       ["KDT", "Vb"], ["B%d" % (6 + (h * dv) // 512)])
                for h in range(4):
                    bank = PS[6 + (h * dv) // 512]
                    c0 = (h * dv) % 512
                    stt(ST[:, h, :], ST[:, h, :], wk(6, n)[:, h, n - 1:n], bank[:, c0:c0 + dv], ALU.mult, ALU.add,
                        ["T6", "B%d" % (6 + (h * dv) // 512), "ST"], ["ST"])
                cp("act", STb[:], ST[:], ["ST"], ["STb"])

            def post(t0, n):
                ti, mc = mcol(t0)
                o_ps = v3(PS[5][:, 0:4 * nvc * 64], n, 4 * nvc)
                g_ps = v3(PS[1][:, 0:256], n)
                if is_h:
                    tt(wk(9, n), o_ps, wk(1, n), ALU.mult, ["B5", "T1"], ["T9"])
                    act(SQb[:, 0:4, 0:n], wk(9, n), AF.Square, ["T9"], ["SQb"])
                    mm(PS[7][:, 0:n], [(cb("ones"), SQb[:, h, 0:n]) for h in range(4)], ["SQb", "CSTb"], ["B7"])
                    act(RSn[:, 0, 0:n], PS[7][:, 0:n], AF.Ln, ["B7"], ["RSn"], bias=RMS_EPS, scale=1.0 / 512)
                    act(RSn[:, 0, 0:n], RSn[:, 0, 0:n], AF.Exp, ["RSn"], ["RSn"], scale=-0.5)
                    tt(wk(9, n), wk(9, n), RSn[:, 0:1, 0:n].to_broadcast([128, 4, n]), ALU.mult, ["T9", "RSn"], ["T9"])
                    tt(MIXT[:, 0:4, mc:mc + n], wk(9, n), ppc("hgrn_norm", idx).unsqueeze(2).to_broadcast([128, 4, n]),
                       ALU.mult, ["T9", "PP"], ["MIXT"])
                else:
                    act(SQb[:, :, 0:n], o_ps, AF.Square, ["B5"], ["SQb"])
                    s_ps = v3(PS[7][:, 0:256], n)
                    for h in range(4):
                        mm(s_ps[:, h, :], [(cb("ones"), SQb[:, h * 2 + vc, 0:n]) for vc in range(2)],
                           ["SQb", "CSTb"], ["B7"])
                    act(RSn[:, :, 0:n], s_ps, AF.Ln, ["B7"], ["RSn"], bias=RMS_EPS, scale=1.0 / 256)
                    act(RSn[:, :, 0:n], RSn[:, :, 0:n], AF.Exp, ["RSn"], ["RSn"], scale=-0.5)
                    o4 = o_ps.rearrange("p (h v) t -> p h v t", v=2)
                    m4 = MIXT[:, 0:8, mc:mc + n].rearrange("p (h v) t -> p h v t", v=2)
                    for vc in range(2):
                        stt(m4[:, :, vc, :], o4[:, :, vc, :], ppc("gla_norm", idx, vc, 1), RSn[:, :, 0:n],
                            ALU.mult, ALU.mult, ["B5", "RSn", "PP"], ["MIXT"])

            def sample_update():
                n = NS
                st_in = (sth_d if is_h else stg_d).ap()[idx]
                st_out = (hs_d if is_h else gs_d).ap()[idx]
                o_ps = v3(PS[5][:, 0:4 * nvc * 64], n, 4 * nvc)
                for b in range(NS):
                    selb = cb("sel", rows=NS)[:, b * 128:(b + 1) * 128]
                    if is_h:
                        vb = PS[6 + b % 2]
                        vbn = "B%d" % (6 + b % 2)
                        mm(vb[:, :], [(selb, Vb[0:NS, 0:512])], ["CSTb", "Vb"], [vbn])
                    else:
                        for j in range(2):
                            mm(PS[6 + j][:, :], [(selb, Vb[0:NS, j * 512:(j + 1) * 512])], ["CSTb", "Vb"], ["B%d" % (6 + j)])
                    for hh in range(2):
                        sn = "SS%d" % hh
                        hs = slice(2 * hh, 2 * hh + 2)
                        ssv = SS[:, 0, hs, :]
                        dma("sp", ssv, st_in[b][2 * hh:2 * hh + 2].rearrange("h k v -> k h v"), [], [sn], sn)
                        if is_h:
                            outv = WK[:, 7 + hh, :].rearrange("p (h v) -> p h v", h=2)
                            on = ["T%d" % (7 + hh)]
                        else:
                            outv = WK[:, 7:9, :] if hh == 0 else WK[:, 0:2, :]
                            on = ["T7", "T8"] if hh == 0 else ["T0", "T1"]
                        for h in (2 * hh, 2 * hh + 1):
                            if is_h:
                                vsrc = vb[:, h * 128:(h + 1) * 128]
                                vres = vbn
                            else:
                                vsrc = PS[6 + hh][:, (h % 2) * 256:(h % 2 + 1) * 256]
                                vres = "B%d" % (6 + hh)
                            ts(SS[:, 0, h, :], SS[:, 0, h, :], wk(6, n)[:, h, b:b + 1], None, ALU.mult, None, [sn, "T6"], [sn])
                            stt(outv[:, h % 2, :], vsrc, KS[:, h, b:b + 1], SS[:, 0, h, :], ALU.mult, ALU.add,
                                [vres, "KS", sn], on)
                        dma("pool", st_out[b][2 * hh:2 * hh + 2].rearrange("h k v -> k h v"), outv, on, [], "SSo%d" % hh)
                        for h in (2 * hh, 2 * hh + 1):
                            for vc in range(nvc):
                                mm(o_ps[:, h * nvc + vc, b:b + 1], [(outv[:, h % 2, vc * 128:(vc + 1) * 128], QS[:, h, b:b + 1])],
                                   on + ["QS"], ["B5"])

            def tile_prefetch():
                if is_h:
                    outproj_load(W["even_w_out"].ap()[idx][0:512, :], 4)
                else:
                    og_d = W["odd_w_in"].ap()[idx].rearrange("(k p) f -> p k f", p=128)
                    for half in range(2):
                        wn = "FW%d" % half
                        wv = FW[:, half, 0:4096].rearrange("p (k f) -> p k f", k=8)
                        dma("pool", wv, og_d[:, :, 2064 + half * 512:2064 + (half + 1) * 512], [], [wn], wn)

            def finish_tile(ti):
                t0, n = tiles[ti]
                if is_h:
                    outproj(W["even_w_out"].ap()[idx][0:512, :], 4, ti, 0, preloaded=True)
                else:
                    for half in range(2):
                        wn = "FW%d" % half
                        wv = FW[:, half, 0:4096].rearrange("p (k f) -> p k f", k=8)
                        for c4 in range(4):
                            bank = PS[c4 % 2]
                            bn = "B%d" % (c4 % 2)
                            mm(bank[:, 0:n], [(wv[:, k, c4 * 128:(c4 + 1) * 128], Hb[:, k, t0:t0 + n]) for k in range(KD)],
                               [wn, "H%d" % ti], [bn])
                            sg = TV[:, 0:512]
                            act(sg[:, 0:n], bank[:, 0:n], AF.Silu, [bn], ["TV"])
                            tt(MIXT[:, half * 4 + c4, 0:n], MIXT[:, half * 4 + c4, 0:n], sg[:, 0:n], ALU.mult,
                               ["TV", "MIXT"], ["MIXT"])
                        wo = FW[:, half, 0:4096].rearrange("p (j m) -> p j m", j=4)
                        dma("pool", wo, W["odd_w_out"].ap()[idx][half * 512:(half + 1) * 512, :].rearrange("(j p) m -> p j m", p=128),
                            [], [wn], wn)
                    outproj(W["odd_w_out"].ap()[idx], 8, ti, 0, preloaded=True)

            seq = []
            for ti, (t0, n) in enumerate(tiles):
                seq.append(("w", ti, 0))
                c = t0
                while c + CH <= min(t0 + n, TP):
                    seq.append(("c", c, CH))
                    c += CH
                if t0 + n > TP:
                    seq.append(("s", TP, NS))
                seq.append(("f", ti, 0))
            done_proj = set()
            for k, ev in enumerate(seq):
                if ev[0] == "w":
                    tile_prefetch()
                    continue
                if ev[0] == "f":
                    finish_tile(ev[1])
                    continue
                _, t0c, ncc = ev
                smp = ev[0] == "s"
                if k not in done_proj:
                    prep_proj(t0c, ncc)
                prep(t0c, ncc, smp)
                if smp:
                    sample_update()
                else:
                    recur(CH)
                nxt = seq[k + 1] if k + 1 < len(seq) else None
                if nxt is not None and nxt[0] in ("c", "s"):
                    prep_proj(nxt[1], nxt[2])
                    done_proj.add(k + 1)
                post(t0c, ncc)
            pst = (hp_d if is_h else gp_d).ap()[idx]
            dma("sp", pst.rearrange("h k v -> k h v"), ST[:], ["ST"], [], "STo")


        def rwkv_pass(idx, lst):
            C0 = float(np.exp(-0.5))
            win = W["even_w_in"].ap()[idx].rearrange("(k p) f -> p k f", p=128)
            WB = MW[:, 0:14336].rearrange("p (k f) -> p k f", k=8)
            for c0 in range(0, 1792, 512):
                c1 = min(1792, c0 + 512)
                dma("pool", WB[:, :, c0:c1], win[:, :, 2048 + c0:2048 + c1], [], ["MW"], "MW")
            LW = MW[:, 14336:14848]
            G2 = MW[:, 14848:15360]
            dma("pool", LW[0:64, :], W["rwkv_w2"].ap()[idx], [], ["MW"], "MW")
            dma("pool", LW[64:128, :], W["rwkv_a2"].ap()[idx], [], ["MW"], "MW")
            dma("pool", G2, W["rwkv_g2"].ap()[idx], [], ["MW"], "MW")
            FWf = FW.bitcast(F32)
            PB = sb("PB", [128, 14, 65], F32, lst)
            XM = sb("XM", [128, 14, 64], F32, lst)
            PRV = sb("PRV", [128, 14, 16], F32, lst)
            GC = sb("GC", [128, 4], F32, lst)
            THp = sb("THp", [128, 64], BF16, lst)
            XAp = sb("XAp", [128, 64], BF16, lst)
            SGb = sb("SGb", [128, 64], BF16, lst)
            SQb = sb("SQbr", [128, 4, 64], BF16, lst)
            P = sb("P", [128, 4, 128], F32, lst)
            Pb = sb("Pb", [128, 4, 128], BF16, lst)
            memset("dve", PB[:], 0.0, ["PB"])
            memset("dve", THp[:], 0.0, ["THp"])
            memset("dve", XAp[:], 0.0, ["XAp"])
            memset("dve", P[:], 0.0, ["P"])
            memset("dve", Pb[:], 0.0, ["Pb"])
            mu = ppc("rwkv_mu", idx)

            def F(i, n):
                return FWf[:, 1, i * 256:(i + 1) * 256].rearrange("p (h t) -> p h t", h=4)[:, :, 0:n]

            def Fn(i):
                return "F%d" % i

            def v4(ap2, n):
                return ap2.rearrange("p (h t) -> p h t", h=4)[:, :, 0:n]

            def bc(name, n, c0=0, nh=4):
                return ppc(name, idx, c0, nh).unsqueeze(2).to_broadcast([128, nh, n])

            def prep_proj(t0, n):
                tok = slice(t0, t0 + n)
                hr = hres(t0, n)
                p0 = PS[0][:, :].rearrange("p (c t) -> p c t", c=8)[:, :, 0:n]
                p1 = PS[1][:, 0:384].rearrange("p (c t) -> p c t", c=6)[:, :, 0:n]
                for c in range(14):
                    dst = p0[:, c, :] if c < 8 else p1[:, c - 8, :]
                    mm(dst, [(WB[:, k, c * 128:(c + 1) * 128], Hb[:, k, tok]) for k in range(KD)], ["MW"] + hr,
                       ["B0" if c < 8 else "B1"])

            def prep(t0, n, sample):
                p0 = PS[0][:, :].rearrange("p (c t) -> p c t", c=8)[:, :, 0:n]
                p1 = PS[1][:, 0:384].rearrange("p (c t) -> p c t", c=6)[:, :, 0:n]
                cp("act", PB[:, 8:14, 1:1 + n], p1, ["B1"], ["PB"])
                cp("act", PB[:, 0:8, 1:1 + n], p0, ["B0"], ["PB"])
                cur = PB[:, :, 1:1 + n]
                prev = PRV[:, :, 0:n] if sample else PB[:, :, 0:n]
                xm = XM[:, :, 0:n]
                tt(xm, prev, cur, ALU.subtract, ["PB", "PRV"], ["XM"])
                tt(xm, xm, mu.unsqueeze(2).to_broadcast([128, 14, n]), ALU.mult, ["XM", "PP"], ["XM"])
                tt(xm, xm, cur, ALU.add, ["XM", "PB"], ["XM"])
                if not sample:
                    cp("dve", PB[:, :, 0:1], PB[:, :, n:n + 1], ["PB", "XM"], ["PB"])
                xr, xk, xv = XM[:, 0:4, 0:n], XM[:, 4:8, 0:n], XM[:, 8:12, 0:n]
                sigmoid_chain(F(0, n)[0:64, 0, :], XM[0:64, 12, 0:n], F(0, n)[0:64, 1, :], ["XM"], [Fn(0)], Fn(0), scale_in=2.0)
                cp("dve", XAp[64:128, 0:n], XM[64:128, 12, 0:n], ["XM"], ["XAp"])
                tt(F(8, n), xk, bc("rwkv_kk", n), ALU.mult, ["XM", "PP"], [Fn(8)])
                tt(SQb[:, :, 0:n], F(8, n), F(8, n), ALU.mult, [Fn(8)], ["SQb"])
                wv_ps = v4(PS[2][:, 0:256], n)
                a_ps = v4(PS[2][:, 256:512], n)
                g_ps = v4(PS[3][:, 0:256], n)
                ss_ps = v4(PS[3][:, 256:512], n)
                for c in range(4):
                    mm(a_ps[:, c, :], [(LW[:, c * 128:(c + 1) * 128], XAp[:, 0:n])], ["MW", "XAp"], ["B2"])
                for c in range(4):
                    mm(ss_ps[:, c, :], [(cb("blk"), SQb[:, c, 0:n])], ["CSTb", "SQb"], ["B3"])
                ts(THp[0:64, 0:n], F(0, n)[0:64, 0, :], 2.0, -1.0, ALU.mult, ALU.add, [Fn(0)], ["THp"])
                for c in range(4):
                    mm(wv_ps[:, c, :], [(LW[:, c * 128:(c + 1) * 128], THp[:, 0:n])], ["MW", "THp"], ["B2"])
                sigmoid_chain(F(0, n)[:, 3, :], XM[:, 13, 0:n], F(0, n)[:, 2, :], ["XM"], ["F0g"], "F0g")
                cp("act", SGb[:, 0:n], F(0, n)[:, 3, :], ["F0g"], ["SGb"])
                for c in range(4):
                    mm(g_ps[:, c, :], [(G2[:, c * 128:(c + 1) * 128], SGb[:, 0:n])], ["MW", "SGb"], ["B3"])
                tt(F(2, n), a_ps, bc("rwkv_a0", n), ALU.add, ["B2", "PP"], [Fn(2)])
                tt(F(1, n), wv_ps, bc("rwkv_w0", n), ALU.add, ["B2", "PP"], [Fn(1)])
                sigmoid_chain(F(1, n), F(1, n), F(3, n), [Fn(1)], [Fn(1)], Fn(3))
                act(F(9, n), ss_ps, AF.Ln, ["B3"], [Fn(9)], bias=1e-18)
                act(F(9, n), F(9, n), AF.Exp, [Fn(9)], [Fn(9)], scale=-0.5)
                sigmoid_chain(F(2, n), F(2, n), F(10, n), [Fn(2)], [Fn(2)], Fn(10))
                tt(F(8, n), F(8, n), F(9, n), ALU.mult, [Fn(8), Fn(9)], [Fn(8)])
                if sample:
                    act(F(5, n), F(1, n), AF.Exp, [Fn(1)], [Fn(5)], scale=-C0)
                else:
                    for c in range(4):
                        S.op("dve", lambda e, c=c: e.tensor_tensor_scan(out=F(3, n)[:, c, :], data0=ONESf[:, 0:n],
                                                                      data1=F(1, n)[:, c, :], initial=0.0,
                                                                      op0=ALU.mult, op1=ALU.add),
                             [Fn(1), "ONESf"], [Fn(3)])
                    tt(F(4, n), F(3, n), F(1, n), ALU.subtract, [Fn(3), Fn(1)], [Fn(4)])
                    act(F(5, n), F(3, n), AF.Exp, [Fn(3)], [Fn(5)], scale=-C0)
                    act(F(6, n), F(3, n), AF.Exp, [Fn(3)], [Fn(6)], scale=C0)
                    act(F(4, n), F(4, n), AF.Exp, [Fn(4)], [Fn(4)], scale=-C0)
                cp("act", F(11, n), g_ps, ["B3"], [Fn(11)])
                for c in range(4):
                    ts(F(9, n)[:, c, :], F(2, n)[:, c, :], -1.0, ppc("rwkv_ka", idx, c, 1), ALU.add, ALU.mult,
                       [Fn(2), "PP"], [Fn(9)])
                stt(F(9, n), F(9, n), 1.0, xk, ALU.add, ALU.mult, [Fn(9), "XM"], [Fn(9)])
                tt(F(10, n), F(8, n), F(2, n), ALU.mult, [Fn(8), Fn(2)], [Fn(10)])
                tt(F(0, n), xr, F(9, n), ALU.mult, ["XM", Fn(9)], [Fn(0), "F0g"])
                tt(SQb[:, :, 0:n], F(0, n), bc("rwkv_rk", n), ALU.mult, [Fn(0), "PP"], ["SQb"])
                bs_ps = v4(PS[4][:, 0:256], n)
                for c in range(4):
                    mm(bs_ps[:, c, :], [(cb("blk"), SQb[:, c, 0:n])], ["CSTb", "SQb"], ["B4"])
                if not sample:
                    cp("dve", GC[:, :], F(5, n)[:, :, n - 1], [Fn(5)], ["GC"])
                tt(F(7, n), bs_ps, xv, ALU.mult, ["B4", "XM"], [Fn(7)])

            with contextlib.ExitStack() as pst:
                AR = sb("ARbd", [128, 4, 2, 128], BF16, pst)
                Bd = sb("Bbd", [128, 4, 128], BF16, pst)
                Kd = sb("Kbd", [128, 4, 128], BF16, pst)
                Vd = sb("Vbd", [128, 4, 128], BF16, pst)
                BT = sb("BTbd", [128, 4, 128], BF16, pst)
                KT = sb("KTbd", [128, 4, 128], BF16, pst)
                VT = sb("VTbd", [128, 4, 128], BF16, pst)
                NK = sb("NK", [128, 4, 2, 128], BF16, pst)
                NTK = sb("NTK", [128, 4, 2, 256], BF16, pst)
                Wb = sb("Wb", [128, 4, 128], BF16, pst)
                Ub = sb("Ub", [128, 4, 128], BF16, pst)
                YFb = sb("YFb", [128, 4, 64], BF16, pst)
                for t_ in (AR, Bd, Kd, Vd):
                    memset("dve", t_[:], 0.0, [])
                S.barrier()
                SCA = MIXT[:, 0:2, :].rearrange("p a (b c) -> p (a b) c", b=2)
                SCB = MIXT[:, 2:4, :].rearrange("p a (b c) -> p (a b) c", b=2)
                SCa = [SCA[:, c, :] for c in range(4)]
                SCb = [SCB[:, c, :] for c in range(4)]

                def chunk(t0):
                    n = CH
                    prep(t0, n, False)
                    xr, xv = XM[:, 0:4, 0:n], XM[:, 8:12, 0:n]
                    for hf in range(2):
                        ps_ = slice(hf * 64, hf * 64 + 64)
                        cs_ = slice(hf * 64, hf * 64 + 64)
                        stt(AR[ps_, :, 0, cs_], F(8, n)[ps_], -1.0, F(4, n)[ps_], ALU.mult, ALU.mult, [Fn(8), Fn(4)], ["AR"])
                        tt(AR[ps_, :, 1, cs_], xr[ps_], F(5, n)[ps_], ALU.mult, ["XM", Fn(5)], ["AR"])
                        tt(Bd[ps_, :, cs_], F(10, n)[ps_], F(6, n)[ps_], ALU.mult, [Fn(10), Fn(6)], ["Bd"])
                        tt(Kd[ps_, :, cs_], F(9, n)[ps_], F(6, n)[ps_], ALU.mult, [Fn(9), Fn(6)], ["Kd"])
                        cp("act", Vd[ps_, :, cs_], xv[ps_], ["XM"], ["Vd"])
                    for c in range(4):
                        tr(PSb[5][:, c * 128:(c + 1) * 128], Bd[:, c, :], cb("ident"), ["Bd", "CSTb"], ["B5"])
                    for c in range(4):
                        tr(PSb[5][:, 512 + c * 128:512 + (c + 1) * 128], Kd[:, c, :], cb("ident"), ["Kd", "CSTb"], ["B5"])
                    for c in range(4):
                        tr(PSb[6][:, c * 128:(c + 1) * 128], Vd[:, c, :], cb("ident"), ["Vd", "CSTb"], ["B6"])
                    cp("act", BT[:].rearrange("p a b -> p (a b)"), PSb[5][:, 0:512], ["B5"], ["BT"])
                    cp("act", KT[:].rearrange("p a b -> p (a b)"), PSb[5][:, 512:1024], ["B5"], ["KT"])
                    cp("act", VT[:].rearrange("p a b -> p (a b)"), PSb[6][:, 0:512], ["B6"], ["VT"])
                    m2 = cf("mt2").unsqueeze(1).to_broadcast([128, 2, 256])
                    for pp_ in range(2):
                        for q_ in range(2):
                            c = 2 * pp_ + q_
                            arv = AR[:, c, :, :].rearrange("p a b -> p (a b)")
                            mm(PS[7][:, q_ * 256:(q_ + 1) * 256], [(Bd[:, c, :], arv)], ["Bd", "AR"], ["B7"])
                            mm(PS[6][:, q_ * 256:(q_ + 1) * 256], [(Kd[:, c, :], arv)], ["Kd", "AR"], ["B6"])
                        tt(SCA[:, 2 * pp_:2 * pp_ + 2, :], PS[7][:, :].rearrange("p (a b) -> p a b", a=2), m2, ALU.mult,
                           ["B7", "CSTf"], ["SCa%d" % (2 * pp_), "SCa%d" % (2 * pp_ + 1)])
                        tt(SCB[:, 2 * pp_:2 * pp_ + 2, :], PS[6][:, :].rearrange("p (a b) -> p a b", a=2), m2, ALU.mult,
                           ["B6", "CSTf"], ["SCb%d" % (2 * pp_), "SCb%d" % (2 * pp_ + 1)])
                    for c in range(4):
                        mm(PS[5][:, c * 128:(c + 1) * 128], [(AR[:, c, 0, :], Bd[:, c, :])], ["AR", "Bd"], ["B5"])
                    tt(NK[:, :, 0, :], PS[5][:, :].rearrange("p (a b) -> p a b", a=4),
                       cf("mstr").unsqueeze(1).to_broadcast([128, 4, 128]), ALU.mult, ["B5", "CSTf"],
                       ["NK%d_0" % c for c in range(4)])
                    cp("act", NTK[:, :, 0, 0:128], SCA[:, :, 0:128], ["SCa%d" % c for c in range(4)],
                       ["NTK%d_0" % c for c in range(4)])
                    cp("act", NTK[:, :, 0, 128:256], cb("ident").unsqueeze(1).to_broadcast([128, 4, 128]), ["CSTb"],
                       ["NTK%d_0" % c for c in range(4)])
                    for k in range(6):
                        pa, pn = k % 2, (k + 1) % 2
                        for c in range(4):
                            bn = "B%d" % c
                            last = k == 5
                            if last:
                                mm(PS[c][:, 128:256], [(NK[:, c, pa, :], NTK[:, c, pa, 128:256])],
                                   ["NK%d_%d" % (c, pa), "NTK%d_%d" % (c, pa)], [bn])
                            else:
                                mm(PS[c][:, 0:256], [(NK[:, c, pa, :], NTK[:, c, pa, :])],
                                   ["NK%d_%d" % (c, pa), "NTK%d_%d" % (c, pa)], [bn])
                                mm(PS[c][:, 256:384], [(NTK[:, c, pa, 0:128], NK[:, c, pa, :])],
                                   ["NK%d_%d" % (c, pa), "NTK%d_%d" % (c, pa)], [bn])
                            tt(NTK[:, c, pn, 128:256], NTK[:, c, pa, 128:256], PS[c][:, 128:256], ALU.add,
                               ["NTK%d_%d" % (c, pa), bn], ["NTK%d_%d" % (c, pn)])
                            if not last:
                                cp("act", NTK[:, c, pn, 0:128], PS[c][:, 0:128], [bn], ["NTK%d_%d" % (c, pn)])
                                cp("act", NK[:, c, pn, :], PS[c][:, 256:384], [bn], ["NK%d_%d" % (c, pn)])
                    for c in range(4):
                        TTc = NTK[:, c, 0, 128:256]
                        mm(PS[4][:, c * 128:(c + 1) * 128], [(AR[:, c, 0, :], Pb[:, c, :]), (SCb[c][:, 0:128], VT[:, c, :])],
                           ["AR", "Pb", "SCb%d" % c, "VT"], ["B4"])
                    cp("act", Wb[:].rearrange("p a b -> p (a b)"), PS[4][:, :], ["B4"], ["Wb"])
                    for c in range(4):
                        mm(PS[5][:, c * 128:(c + 1) * 128], [(NTK[:, c, 0, 128:256], Wb[:, c, :])], ["NTK%d_0" % c, "Wb"], ["B5"])
                    cp("act", Ub[:].rearrange("p a b -> p (a b)"), PS[5][:, :], ["B5"], ["Ub"])
                    for c in range(4):
                        mm(PS[6][:, c * 128:(c + 1) * 128],
                           [(Pb[:, c, :], AR[:, c, 1, :]), (Ub[:, c, :], SCa[c][:, 128:256]), (VT[:, c, :], SCb[c][:, 128:256])],
                           ["Pb", "AR", "Ub", "SCa%d" % c, "SCb%d" % c, "VT"], ["B6"])
                    for c in range(4):
                        mm(PS[7][:, c * 128:(c + 1) * 128], [(BT[:, c, :], Ub[:, c, :]), (KT[:, c, :], VT[:, c, :])],
                           ["BT", "KT", "Ub", "VT"], ["B7"])
                    yv = PS[6][:, :].rearrange("p (c t) -> p c t", c=4)
                    cp("act", F(6, n)[0:64], yv[0:64, :, 0:64], ["B6"], [Fn(6)])
                    cp("act", F(6, n)[64:128], yv[64:128, :, 64:128], ["B6"], [Fn(6)])
                    tt(P[:], P[:], PS[7][:, :].rearrange("p (a b) -> p a b", a=4), ALU.add, ["P", "B7"], ["P"])
                    tt(P[:], P[:], GC[:, :].unsqueeze(2).to_broadcast([128, 4, 128]), ALU.mult, ["P", "GC"], ["P"])
                    cp("act", Pb[:], P[:], ["P"], ["Pb"])

                def post(t0, n):
                    ti, mc = mcol(t0)
                    cp("act", YFb[:, :, 0:n], F(6, n), [Fn(6)], ["YFb"])
                    m_ps = v4(PS[2][:, 0:256], n)
                    for c in range(4):
                        mm(m_ps[:, c, :], [(cb("blk"), YFb[:, c, 0:n])], ["CSTb", "YFb"], ["B2"])
                    stt(F(5, n), m_ps, -1.0 / 64, F(6, n), ALU.mult, ALU.add, ["B2", Fn(6)], [Fn(5)])
                    act(YFb[:, :, 0:n], F(5, n), AF.Square, [Fn(5)], ["YFb"])
                    v_ps = v4(PS[3][:, 0:256], n)
                    for c in range(4):
                        mm(v_ps[:, c, :], [(cb("blk"), YFb[:, c, 0:n])], ["CSTb", "YFb"], ["B3"])
                    act(F(4, n), v_ps, AF.Ln, ["B3"], [Fn(4)], bias=GN_EPS, scale=1.0 / 64)
                    act(F(4, n), F(4, n), AF.Exp, [Fn(4)], [Fn(4)], scale=-0.5)
                    tt(F(5, n), F(5, n), F(4, n), ALU.mult, [Fn(5), Fn(4)], [Fn(5)])
                    tt(F(5, n), F(5, n), bc("rwkv_ln_w", n), ALU.mult, [Fn(5), "PP"], [Fn(5)])
                    tt(F(5, n), F(5, n), bc("rwkv_ln_b", n), ALU.add, [Fn(5), "PP"], [Fn(5)])
                    tt(F(5, n), F(5, n), F(7, n), ALU.add, [Fn(5), Fn(7)], [Fn(5)])
                    tt(MIXT[:, 4:8, mc:mc + n], F(5, n), F(11, n), ALU.mult, [Fn(5), Fn(11)], ["MIXT"])

                XSCR = [FW[:, 0, 4096 + c * 384:4096 + (c + 1) * 384] for c in range(4)]
                seq = []
                for ti, (t0, n) in enumerate(tiles):
                    seq.append(("w", ti))
                    c_ = t0
                    while c_ + CH <= min(t0 + n, TP):
                        seq.append(("c", c_))
                        c_ += CH
                    if t0 + n <= TP:
                        seq.append(("f", ti))
                done_proj = set()
                for k, ev in enumerate(seq):
                    if ev[0] == "w":
                        outproj_load(W["even_w_out"].ap()[idx][512:1024, :], 4)
                        continue
                    if ev[0] == "f":
                        outproj(W["even_w_out"].ap()[idx][512:1024, :], 4, ev[1], 4, preloaded=True)
                        continue
                    if k not in done_proj:
                        prep_proj(ev[1], CH)
                    chunk(ev[1])
                    nxt = seq[k + 1] if k + 1 < len(seq) else None
                    if nxt is not None and nxt[0] == "c":
                        prep_proj(nxt[1], CH)
                        done_proj.add(k + 1)
                    post(ev[1], CH)
                sh_f = F(4, 64)[:, :, :].rearrange("p a b -> p (a b)")
                cp("dve", sh_f[:, 0:14], PB[:, :, 0], ["PB"], [Fn(4)])
                tr(PS[4][0:14, 0:128], sh_f[:, 0:14], CSTf[:, 0:128], [Fn(4), "CSTf"], ["B4"])
                cp("act", sh_f[0:14, 128:256], PS[4][0:14, 0:128], ["B4"], [Fn(4)])
                dma("sp", sp_d.ap()[idx].rearrange("(c p) -> c p", p=128), sh_f[0:14, 128:256], [Fn(4)], [], "SHo")
                for c in range(4):
                    tr(PS[c][:, 0:128], P[:, c, :], CSTf[:, 0:128], ["P", "CSTf"], ["B%d" % c])
                    pt = F(c, 64)[:, :, :].rearrange("p a b -> p (a b)")
                    cp("act", pt[:, 0:128], PS[c][:, 0:128], ["B%d" % c], [Fn(c)])
                    for hf in range(2):
                        dma("sp", rp_d.ap()[idx][2 * c + hf], pt[hf * 64:(hf + 1) * 64, hf * 64:(hf + 1) * 64],
                            [Fn(c)], [], "RPo")
                S.barrier()

            with contextlib.ExitStack() as sst:
                SS_A = sb("SSr", [128, NS, 64], F32, sst)
                T1 = sb("T1r", [128, NS, 64], F32, sst)
                RB = sb("RBr", [128, NS, 64], BF16, sst)
                XT2 = sb("XT2", [128, 1792], F32, sst)
                SA = sb("SAr", [128, NS], F32, sst)
                n = NS
                dma("sp", XT2[0:NS, :], sts_d.ap()[idx], [], ["XT2"], "XT2")
                for c in range(14):
                    bank = PS[c % 2]
                    tr(bank[:, 0:NS], XT2[0:NS, c * 128:(c + 1) * 128], CSTf[0:NS, 0:NS], ["XT2", "CSTf"], ["B%d" % (c % 2)])
                    cp("act", PRV[:, c, 0:NS], bank[:, 0:NS], ["B%d" % (c % 2)], ["PRV"])
                prep_proj(TP, n)
                prep(TP, n, True)
                for c in range(14):
                    bank = PS[4 + (c // 4) % 4]
                    bn = "B%d" % (4 + (c // 4) % 4)
                    tr(bank[0:NS, (c % 4) * 128:(c % 4 + 1) * 128], PB[:, c, 1:1 + NS], CSTf[:, 0:128], ["PB", "CSTf"], [bn])
                    if c % 4 == 3 or c == 13:
                        c0 = (c // 4) * 4
                        w_ = (c - c0 + 1) * 128
                        cp("act", XT2[0:NS, c0 * 128:c0 * 128 + w_], bank[0:NS, 0:w_], [bn], ["XT2"])
                dma("sp", ss_d.ap()[idx], XT2[0:NS, :], ["XT2"], [], "SHo")
                xr, xv = XM[:, 0:4, 0:n], XM[:, 8:12, 0:n]
                idt = cf("idt").unsqueeze(1).to_broadcast([128, NS, 64])

                def bcast_rows(q, qn, c):
                    tt(RB[:, :, :], idt, q[:, c, :].unsqueeze(2).to_broadcast([128, NS, 64]), ALU.mult, ["CSTf", qn], ["RB"])
                    for hh in range(2):
                        mm(PS[2 + hh][:, :], [(cb("blk"), RB[:, hh * 8:(hh + 1) * 8, :].rearrange("p a b -> p (a b)"))],
                           ["CSTb", "RB"], ["B%d" % (2 + hh)])

                def psv(hh):
                    return PS[2 + hh][:, :].rearrange("p (a b) -> p a b", a=8)

                SS_B = FWf[:, 0, 2048:3072].rearrange("p (b j) -> p b j", b=NS)
                for c in range(4):
                    SS = SS_A if c % 2 == 0 else SS_B
                    ssn = "SSr%d" % (c % 2)
                    dma("sp", SS[:], str_d.ap()[idx][:, 2 * c:2 * c + 2].rearrange("b h i j -> (h i) b j"), [], [ssn], ssn)
                    bcast_rows(F(8, n), Fn(8), c)
                    for hh in range(2):
                        tt(T1[:, hh * 8:(hh + 1) * 8, :], SS[:, hh * 8:(hh + 1) * 8, :], psv(hh), ALU.mult,
                           [ssn, "B%d" % (2 + hh)], ["T1"])
                    S.op("dve", lambda e: e.tensor_reduce(out=SA[:, :], in_=T1[:, :, :], axis=AX.X, op=ALU.add, negate=True),
                         ["T1"], ["SA"])
                    bcast_rows(F(5, n), Fn(5), c)
                    for hh in range(2):
                        tt(SS[:, hh * 8:(hh + 1) * 8, :], SS[:, hh * 8:(hh + 1) * 8, :], psv(hh), ALU.mult,
                           [ssn, "B%d" % (2 + hh)], [ssn])
                    bcast_rows(F(10, n), Fn(10), c)
                    for hh in range(2):
                        tt(T1[:, hh * 8:(hh + 1) * 8, :], psv(hh), SA[:, hh * 8:(hh + 1) * 8].unsqueeze(2).to_broadcast([128, 8, 64]),
                           ALU.mult, ["SA", "B%d" % (2 + hh)], ["T1"])
                    tt(SS[:], SS[:], T1[:], ALU.add, [ssn, "T1"], [ssn])
                    bcast_rows(F(9, n), Fn(9), c)
                    for hh in range(2):
                        tt(T1[:, hh * 8:(hh + 1) * 8, :], psv(hh), xv[:, c, hh * 8:(hh + 1) * 8].unsqueeze(2).to_broadcast([128, 8, 64]),
                           ALU.mult, ["XM", "B%d" % (2 + hh)], ["T1"])
                    tt(SS[:], SS[:], T1[:], ALU.add, [ssn, "T1"], [ssn])
                    dma("pool", rs_d.ap()[idx][:, 2 * c:2 * c + 2].rearrange("b h i j -> (h i) b j"), SS[:], [ssn], [], "SSro%d" % (c % 2))
                    bcast_rows(xr, "XM", c)
                    for hh in range(2):
                        tt(T1[:, hh * 8:(hh + 1) * 8, :], SS[:, hh * 8:(hh + 1) * 8, :], psv(hh), ALU.mult,
                           [ssn, "B%d" % (2 + hh)], ["T1"])
                    S.op("dve", lambda e, c=c: e.tensor_reduce(out=F(6, n)[:, c, :], in_=T1[:, :, :], axis=AX.X, op=ALU.add),
                         ["T1"], [Fn(6)])
            with contextlib.ExitStack() as pst2:
                YFb = sb("YFb2", [128, 4, 64], BF16, pst2)
                n = NS
                ti, mc = mcol(TP)
                cp("act", YFb[:, :, 0:n], F(6, n), [Fn(6)], ["YFb"])
                m_ps = v4(PS[0][:, 0:256], n)
                for c in range(4):
                    mm(m_ps[:, c, :], [(cb("blk"), YFb[:, c, 0:n])], ["CSTb", "YFb"], ["B0"])
                stt(F(5, n), m_ps, -1.0 / 64, F(6, n), ALU.mult, ALU.add, ["B0", Fn(6)], [Fn(5)])
                act(YFb[:, :, 0:n], F(5, n), AF.Square, [Fn(5)], ["YFb"])
                v_ps = v4(PS[1][:, 0:256], n)
                for c in range(4):
                    mm(v_ps[:, c, :], [(cb("blk"), YFb[:, c, 0:n])], ["CSTb", "YFb"], ["B1"])
                act(F(4, n), v_ps, AF.Ln, ["B1"], [Fn(4)], bias=GN_EPS, scale=1.0 / 64)
                act(F(4, n), F(4, n), AF.Exp, [Fn(4)], [Fn(4)], scale=-0.5)
                tt(F(5, n), F(5, n), F(4, n), ALU.mult, [Fn(5), Fn(4)], [Fn(5)])
                tt(F(5, n), F(5, n), bc("rwkv_ln_w", n), ALU.mult, [Fn(5), "PP"], [Fn(5)])
                tt(F(5, n), F(5, n), bc("rwkv_ln_b", n), ALU.add, [Fn(5), "PP"], [Fn(5)])
                tt(F(5, n), F(5, n), F(7, n), ALU.add, [Fn(5), Fn(7)], [Fn(5)])
                tt(MIXT[:, 4:8, mc:mc + n], F(5, n), F(11, n), ALU.mult, [Fn(5), Fn(11)], ["MIXT"])
                outproj(W["even_w_out"].ap()[idx][512:1024, :], 4, NT - 1, 4, preloaded=True)

        def zero_pad_x():
            memset("dve", X[:, :, 0:PAD], 0.0, ["X0"])

        for l in range(DEPTH):
            if do_ffn and ((l % 2 == 0 and do_even) or (l % 2 == 1 and do_odd)):
                gla_load("hgrn" if l % 2 == 0 else "gla", l // 2)
            if do_ffn:
                with contextlib.ExitStack() as lst:
                    rmsnorm_to_h("norm_ffn1", l, lst)
                    ffn(l, 1, lst)
                    S.barrier()
            if (l % 2 == 0 and do_even) or (l % 2 == 1 and do_odd):
                with contextlib.ExitStack() as lst:
                    rmsnorm_to_h("norm_mix", l, lst)
                    S.barrier()
                if l % 2 == 0:
                    with contextlib.ExitStack() as lst:
                        gla_pass("hgrn", l // 2, lst)
                        S.barrier()
                    with contextlib.ExitStack() as lst:
                        rwkv_pass(l // 2, lst)
                        S.barrier()
                    zero_pad_x()
                else:
                    with contextlib.ExitStack() as lst:
                        gla_pass("gla", l // 2, lst)
                        S.barrier()
            if do_ffn:
                with contextlib.ExitStack() as lst:
                    rmsnorm_to_h("norm_ffn2", l, lst)
                    ffn(l, 2, lst)
                    S.barrier()

        with contextlib.ExitStack() as lst:
            SQ = sb("SQf", [128, 1, KD, 512], BF16, lst)
            RS = sb("RSf", [128, 2, 512], F32, lst)
            YT = sb("YT", [128, 2, D], F32, lst)
            for ti, (t0, n) in enumerate(tiles):
                sl = ti % 2
                act(SQ[:, 0, :, 0:n], X[:, :, t0:t0 + n], AF.Square, ["X%d" % ti], ["SQ0"])
                bank = PS[6 + sl]
                bn = "B%d" % (6 + sl)
                mm(bank[:, 0:n], [(cb("ones"), SQ[:, 0, c, 0:n]) for c in range(KD)], ["SQ0", "CSTb"], [bn])
                act(RS[:, sl, 0:n], bank[:, 0:n], AF.Ln, [bn], ["RS%d" % sl], bias=RMS_EPS, scale=1.0 / D)
                act(RS[:, sl, 0:n], RS[:, sl, 0:n], AF.Exp, ["RS%d" % sl], ["RS%d" % sl], scale=-0.5)
                for c in range(KD):
                    stt(X[:, c, t0:t0 + n], X[:, c, t0:t0 + n], ppc("final_norm", 0, c, 1), RS[:, sl, 0:n],
                        ALU.mult, ALU.mult, ["X%d" % ti, "RS%d" % sl, "PP"], ["X%d" % ti])
            dsts = [(yp_d.ap()[i * 128:(i + 1) * 128, :], 128, PAD + NMETA + i * 128) for i in range(S_LEN // 128)]
            dsts.append((ys_d.ap(), NS, TP))
            for i, (dst, n, t0) in enumerate(dsts):
                sl = i % 2
                for half in range(2):
                    bank = PS[(i * 2 + half) % 4]
                    bn = "B%d" % ((i * 2 + half) % 4)
                    for c4 in range(4):
                        c = half * 4 + c4
                        tr(bank[0:n, c4 * 128:(c4 + 1) * 128], X[:, c, t0:t0 + n], CSTf[:, 0:128],
                           xres(t0, n) + ["CSTf"], [bn])
                    cp("act" if half == 0 else "dve", YT[0:n, sl, half * 512:(half + 1) * 512], bank[0:n, :],
                       [bn], ["YT%d_%d" % (sl, half)])
                dma("sp", dst, YT[0:n, sl, :], ["YT%d_0" % sl, "YT%d_1" % sl], [], "YTo%d" % sl)
        S.emit(st)
    return nc


_NC_CACHE = {}


def kernel(**inputs):
    inputs = {k: np.asarray(v) for k, v in inputs.items()}
    depth = inputs["norm_ffn1"].shape[0]
    B, S_LEN, _ = inputs["x_prompt"].shape
    n_cores = 8
    NS = inputs["x_sample"].shape[0] // n_cores
    key = (S_LEN, NS, depth)
    if key not in _NC_CACHE:
        _NC_CACHE[key] = build(S_LEN, NS, depth)
    nc = _NC_CACHE[key]
    pp = pack_pp(inputs, depth)
    cst = make_cst()
    shared = {"pp": pp, "cst": cst, "meta": np.ascontiguousarray(inputs["meta_tokens"], np.float32)}
    for nm in ("ffn1_gate", "ffn1_up", "ffn1_down", "ffn2_gate", "ffn2_up", "ffn2_down", "even_w_in", "even_w_out",
               "odd_w_in", "odd_w_out", "rwkv_w2", "rwkv_a2", "rwkv_g2", "gla_gate_up"):
        shared[nm] = np.ascontiguousarray(inputs[nm], np.float32)
    in_maps = []
    for c in range(n_cores):
        m = dict(shared)
        m["xp"] = np.ascontiguousarray(inputs["x_prompt"][c])
        m["xs"] = np.ascontiguousarray(inputs["x_sample"][c * NS:(c + 1) * NS, 0, :])
        m["st_hgrn"] = np.ascontiguousarray(inputs["state_hgrn"][:, c * NS:(c + 1) * NS])
        m["st_rwkv"] = np.ascontiguousarray(inputs["state_rwkv"][:, c * NS:(c + 1) * NS])
        m["st_shift"] = np.ascontiguousarray(inputs["state_rwkv_shift"][:, c * NS:(c + 1) * NS])
        m["st_gla"] = np.ascontiguousarray(inputs["state_gla"][:, c * NS:(c + 1) * NS])
        in_maps.append(m)
    res = run_bass_kernel_spmd(nc, in_maps, core_ids=list(range(n_cores)))
    R = res.results
    yp = np.stack([R[c]["y_prompt"] for c in range(n_cores)], 0)
    ys = np.concatenate([R[c]["y_sample"] for c in range(n_cores)], 0)[:, None, :]
    hp = np.stack([R[c]["hgrn_prompt"] for c in range(n_cores)], 1)
    rp = np.stack([R[c]["rwkv_prompt"] for c in range(n_cores)], 1)
    sp = np.stack([R[c]["shift_prompt"] for c in range(n_cores)], 1)
    gp = np.stack([R[c]["gla_prompt"] for c in range(n_cores)], 1)
    hs = np.concatenate([R[c]["hgrn_sample"] for c in range(n_cores)], 1)
    rs = np.concatenate([R[c]["rwkv_sample"] for c in range(n_cores)], 1)
    ss = np.concatenate([R[c]["shift_sample"] for c in range(n_cores)], 1)
    gs = np.concatenate([R[c]["gla_sample"] for c in range(n_cores)], 1)
    return tuple(np.ascontiguousarray(a, dtype=np.float32) for a in (yp, ys, hp, rp, sp, gp, hs, rs, ss, gs))
```
